# Optimizing a Trainium2 kernel written in Bass

```python
import math
import jax, jax.numpy as jnp
from jax import lax
import numpy as np

D_MODEL = 1024
BATCH = 8
SEQ = 2048
DEPTH = 1

D_MIX = D_MODEL
D_GLA = D_MIX // 2
D_MLA = D_MIX - D_GLA

GLA_HEADS = 4
GLA_DV = D_GLA // GLA_HEADS
GLA_DK = GLA_DV // 2
GLA_QK = GLA_HEADS * GLA_DK
GLA_GATE_RANK = 16
GLA_TAU = 16.0
GLA_CHUNK = 64

MLA_HEADS = 4
MLA_DV = D_MLA // MLA_HEADS
MLA_D_NOPE = 128
MLA_D_ROPE = 64
MLA_Q_RANK = 256
MLA_KV_RANK = 128
ROPE_THETA = 10000.0
Q_BLOCK = 128

RMS_EPS = 1e-6
LN_EPS = 1e-5
DEEPNORM_ALPHA = (2.0 * DEPTH) ** 0.25
DEEPNORM_BETA = (8.0 * DEPTH) ** -0.25

IN_SPLITS = (GLA_QK, GLA_QK, D_GLA, D_GLA, GLA_GATE_RANK, GLA_GATE_RANK,
             MLA_Q_RANK, MLA_KV_RANK, MLA_D_ROPE, D_MLA)
IN_COLS = sum(IN_SPLITS)

kernel_name = "hymba_gla_mla_deepnorm_encoder"


def rms_norm(t, g):
    t32 = t.astype(jnp.float32)
    t32 = t32 * lax.rsqrt(jnp.mean(t32 * t32, axis=-1, keepdims=True) + RMS_EPS)
    return (t32 * g.astype(jnp.float32)).astype(t.dtype)


def layer_norm(t, g, b):
    t32 = t.astype(jnp.float32)
    mu = jnp.mean(t32, axis=-1, keepdims=True)
    var = jnp.mean(jnp.square(t32 - mu), axis=-1, keepdims=True)
    y = (t32 - mu) * lax.rsqrt(var + LN_EPS) * g.astype(jnp.float32) + b.astype(jnp.float32)
    return y.astype(t.dtype)


def apply_rope(t, positions):
    half = t.shape[-1] // 2
    inv_freq = ROPE_THETA ** (-jnp.arange(half, dtype=jnp.float32) / half)
    ang = positions.astype(jnp.float32)[..., None] * inv_freq
    ang = ang.reshape(ang.shape[:2] + (1,) * (t.ndim - 3) + (half,))
    cos, sin = jnp.cos(ang), jnp.sin(ang)
    t32 = t.astype(jnp.float32)
    t1, t2 = t32[..., :half], t32[..., half:]
    out = jnp.concatenate([t1 * cos - t2 * sin, t1 * sin + t2 * cos], axis=-1)
    return out.astype(t.dtype)


def gla_chunked(q, k, v, log_a, strict):
    B, S, H, dk = q.shape
    dv = v.shape[-1]
    C = GLA_CHUNK
    N = S // C

    def to_chunks(t):
        return t.reshape(B, N, C, H, t.shape[-1]).transpose(1, 0, 3, 2, 4)

    qc, kc, vc, gc = (to_chunks(t.astype(jnp.float32)) for t in (q, k, v, log_a))
    b = jnp.cumsum(gc, axis=3)
    b_last = b[:, :, :, -1:, :]
    q_dec = qc * jnp.exp(b)
    k_dec = kc * jnp.exp(-b)
    k_tail = kc * jnp.exp(b_last - b)

    mask = jnp.tril(jnp.ones((C, C), dtype=bool), k=-1 if strict else 0)
    attn = jnp.where(mask, jnp.einsum('nbhtd,nbhsd->nbhts', q_dec, k_dec), 0.0)
    o_intra = jnp.einsum('nbhts,nbhsv->nbhtv', attn, vc)

    def step(state, inp):
        qd, kt, vv, bl = inp
        o = jnp.einsum('bhtd,bhdv->bhtv', qd, state)
        state = state * jnp.exp(bl[:, :, 0, :])[..., None] + jnp.einsum('bhsd,bhsv->bhdv', kt, vv)
        return state, o

    s0 = jnp.zeros((B, H, dk, dv), jnp.float32)
    _, o_inter = lax.scan(step, s0, (q_dec, k_tail, vc, b_last))
    o = o_intra + o_inter
    return o.transpose(1, 0, 3, 2, 4).reshape(B, S, H, dv)


def gla_branch(q, k, v, gate, lr_f, lr_b, wg2_f, bg_f, wg2_b, bg_b, norm_g):
    B, S, _ = q.shape
    q = q.reshape(B, S, GLA_HEADS, GLA_DK) * (GLA_DK ** -0.5)
    k = k.reshape(B, S, GLA_HEADS, GLA_DK)
    v = v.reshape(B, S, GLA_HEADS, GLA_DV)
    log_af = (jax.nn.log_sigmoid((lr_f @ wg2_f + bg_f).astype(jnp.float32)) / GLA_TAU
              ).reshape(B, S, GLA_HEADS, GLA_DK)
    log_ab = (jax.nn.log_sigmoid((lr_b @ wg2_b + bg_b).astype(jnp.float32)) / GLA_TAU
              ).reshape(B, S, GLA_HEADS, GLA_DK)
    flip = lambda t: t[:, ::-1]
    o_fwd = gla_chunked(q, k, v, log_af, strict=False)
    o_bwd = flip(gla_chunked(flip(q), flip(k), flip(v), flip(log_ab), strict=True))
    o = rms_norm(o_fwd + o_bwd, norm_g)
    o = o.reshape(B, S, D_GLA).astype(gate.dtype)
    return o * jax.nn.silu(gate)


def mla_branch(c_q, c_kv, k_rope, gate, positions, q_norm_g, w_uq, kv_norm_g, w_ukv):
    B, S, _ = c_q.shape
    q = (rms_norm(c_q, q_norm_g) @ w_uq).reshape(B, S, MLA_HEADS, MLA_D_NOPE + MLA_D_ROPE)
    kv = (rms_norm(c_kv, kv_norm_g) @ w_ukv).reshape(B, S, MLA_HEADS, MLA_D_NOPE + MLA_DV)
    scale = (MLA_D_NOPE + MLA_D_ROPE) ** -0.5
    q_nope = q[..., :MLA_D_NOPE] * scale
    q_rope = apply_rope(q[..., MLA_D_NOPE:], positions) * scale
    k_nope, v = kv[..., :MLA_D_NOPE], kv[..., MLA_D_NOPE:]
    k_r = apply_rope(k_rope, positions)

    nb = S // Q_BLOCK
    to_blocks = lambda t: t.reshape((B, nb, Q_BLOCK) + t.shape[2:]).transpose(1, 0, 2, 3, 4)

    def attend(blk):
        qn, qr = blk
        s = (jnp.einsum('bqhd,bkhd->bhqk', qn, k_nope, preferred_element_type=jnp.float32)
             + jnp.einsum('bqhr,bkr->bhqk', qr, k_r, preferred_element_type=jnp.float32))
        p = jax.nn.softmax(s, axis=-1)
        return jnp.einsum('bhqk,bkhv->bqhv', p.astype(v.dtype), v)

    o = lax.map(attend, (to_blocks(q_nope), to_blocks(q_rope)))
    o = o.transpose(1, 0, 2, 3, 4).reshape(B, S, D_MLA)
    return o * jax.nn.silu(gate)


def hybrid_mixer(x, positions, w_in, wg2_f, bg_f, wg2_b, bg_b, gla_norm_g,
                 q_norm_g, w_uq, kv_norm_g, w_ukv, w_out):
    h = x @ w_in
    idx = np.cumsum(IN_SPLITS)[:-1].tolist()
    (g_q, g_k, g_v, g_gate, g_lr_f, g_lr_b,
     m_cq, m_ckv, m_kr, m_gate) = jnp.split(h, idx, axis=-1)
    o_a = gla_branch(g_q, g_k, g_v, g_gate, g_lr_f, g_lr_b, wg2_f, bg_f, wg2_b, bg_b, gla_norm_g)
    o_b = mla_branch(m_cq, m_ckv, m_kr, m_gate, positions, q_norm_g, w_uq, kv_norm_g, w_ukv)
    return jnp.concatenate([o_a, o_b], axis=-1) @ w_out


def setup_inputs(seed: int = 0) -> dict:
    key = jax.random.key(seed)
    ks = jax.random.split(key, 20)
    nrm = lambda k, shape, fan_in: jax.random.normal(k, shape, jnp.float32) * (fan_in ** -0.5)

    x = jax.random.normal(ks[0], (BATCH, SEQ, D_MODEL), jnp.float32)
    positions = jnp.broadcast_to(jnp.arange(SEQ, dtype=jnp.int32)[None, :], (BATCH, SEQ))

    col_scale = [1.0, 1.0, DEEPNORM_BETA, 1.0, 1.0, 1.0, 1.0, 1.0, 1.0, 1.0]
    pieces = [nrm(k, (DEPTH, D_MODEL, n), D_MODEL) * s
              for k, n, s in zip(jax.random.split(ks[1], len(IN_SPLITS)), IN_SPLITS, col_scale)]
    w_in = jnp.concatenate(pieces, axis=-1)

    gla_wg2_fwd = nrm(ks[2], (DEPTH, GLA_GATE_RANK, GLA_QK), GLA_GATE_RANK)
    gla_bg_fwd = 0.1 * jax.random.normal(ks[3], (DEPTH, GLA_QK), jnp.float32)
    gla_wg2_bwd = nrm(ks[4], (DEPTH, GLA_GATE_RANK, GLA_QK), GLA_GATE_RANK)
    gla_bg_bwd = 0.1 * jax.random.normal(ks[5], (DEPTH, GLA_QK), jnp.float32)
    gla_norm_g = 1.0 + 0.02 * jax.random.normal(ks[6], (DEPTH, GLA_DV), jnp.float32)

    mla_q_norm_g = 1.0 + 0.02 * jax.random.normal(ks[7], (DEPTH, MLA_Q_RANK), jnp.float32)
    mla_w_uq = nrm(ks[8], (DEPTH, MLA_Q_RANK, MLA_HEADS * (MLA_D_NOPE + MLA_D_ROPE)), MLA_Q_RANK)
    mla_kv_norm_g = 1.0 + 0.02 * jax.random.normal(ks[9], (DEPTH, MLA_KV_RANK), jnp.float32)
    ukv = nrm(ks[10], (DEPTH, MLA_KV_RANK, MLA_HEADS, MLA_D_NOPE + MLA_DV), MLA_KV_RANK)
    v_scale = jnp.concatenate([jnp.ones((MLA_D_NOPE,), jnp.float32),
                               jnp.full((MLA_DV,), DEEPNORM_BETA, jnp.float32)])
    mla_w_ukv = (ukv * v_scale).reshape(DEPTH, MLA_KV_RANK, MLA_HEADS * (MLA_D_NOPE + MLA_DV))

    w_out = nrm(ks[11], (DEPTH, D_MIX, D_MODEL), D_MIX) * DEEPNORM_BETA
    ln_g = 1.0 + 0.02 * jax.random.normal(ks[12], (DEPTH, D_MODEL), jnp.float32)
    ln_b = 0.02 * jax.random.normal(ks[13], (DEPTH, D_MODEL), jnp.float32)

    return {"x": x, "positions": positions, "w_in": w_in,
            "gla_wg2_fwd": gla_wg2_fwd, "gla_bg_fwd": gla_bg_fwd,
            "gla_wg2_bwd": gla_wg2_bwd, "gla_bg_bwd": gla_bg_bwd,
            "gla_norm_g": gla_norm_g, "mla_q_norm_g": mla_q_norm_g, "mla_w_uq": mla_w_uq,
            "mla_kv_norm_g": mla_kv_norm_g, "mla_w_ukv": mla_w_ukv,
            "w_out": w_out, "ln_g": ln_g, "ln_b": ln_b}


def reference(x, positions, w_in, gla_wg2_fwd, gla_bg_fwd, gla_wg2_bwd, gla_bg_bwd,
              gla_norm_g, mla_q_norm_g, mla_w_uq, mla_kv_norm_g, mla_w_ukv,
              w_out, ln_g, ln_b):
    for layer in range(DEPTH):
        mixed = hybrid_mixer(x, positions, w_in[layer],
                             gla_wg2_fwd[layer], gla_bg_fwd[layer],
                             gla_wg2_bwd[layer], gla_bg_bwd[layer], gla_norm_g[layer],
                             mla_q_norm_g[layer], mla_w_uq[layer],
                             mla_kv_norm_g[layer], mla_w_ukv[layer], w_out[layer])
        x = layer_norm(DEEPNORM_ALPHA * x + mixed, ln_g[layer], ln_b[layer])
    return x
```

```python
import math
from contextlib import ExitStack

import numpy as np
import concourse.bass as bass
import concourse.mybir as mybir
from concourse.bass_utils import run_bass_kernel_spmd

F32 = mybir.dt.float32
BF16 = mybir.dt.bfloat16
I32 = mybir.dt.int32
AF = mybir.ActivationFunctionType
ALU = mybir.AluOpType

D_MODEL = 1024
BATCH = 8
SEQ = 2048
NT = SEQ // 128
NG = SEQ // 512
RMS_EPS = 1e-6
LN_EPS = 1e-5
ALPHA = 2.0 ** 0.25
ATT_SCALE = 192.0 ** -0.5
IN_SPLITS = (256, 256, 512, 512, 16, 16, 256, 128, 64, 512)


class Tile:
    __slots__ = ("name", "w", "r", "inherit", "lo", "hi")

    def __init__(self, name, inherit=None, lo=0, hi=0):
        self.name = name
        self.w = {}
        self.r = {}
        self.inherit = list(inherit) if inherit else []
        self.lo, self.hi = lo, hi


class DmaSem:
    def __init__(self, sem, name):
        self.sem, self.name, self.count = sem, name, 0


class Op:
    __slots__ = ("engine", "fn", "idx", "signal", "semval", "waits", "dmasem", "dmacount", "is_dma")


class Sched:
    ENGS = ("pe", "act", "dve", "pool", "sp")

    def __init__(self, nc, es):
        self.nc = nc
        self.es = es
        self.ops = {e: [] for e in self.ENGS}
        self.sems = {e: es.enter_context(nc.semaphore("sem_" + e)) for e in ("pe", "act", "dve", "pool")}
        self.ndma = 0

    def dmasem(self, name):
        return DmaSem(self.es.enter_context(self.nc.semaphore("dq_" + name)), name)

    def _collect(self, reads, writes):
        deps = []
        for t in reads:
            deps.extend(t.w.values())
            deps.extend(t.inherit)
        for t in writes:
            deps.extend(t.w.values())
            deps.extend(t.r.values())
            deps.extend(t.inherit)
        return deps

    def op(self, engine, fn, reads=(), writes=()):
        o = Op()
        o.engine, o.fn, o.idx = engine, fn, len(self.ops[engine])
        o.signal, o.semval, o.is_dma, o.dmasem, o.dmacount = False, None, False, None, 0
        deps = self._collect(reads, writes)
        o.waits = self._filter(o, deps)
        self.ops[engine].append(o)
        me = ("eng", engine, o.idx)
        for t in reads:
            t.r[engine] = me
        for t in writes:
            t.w = {engine: me}
            t.r = {}
        return o

    def dma(self, queue, fn, dsem, reads=(), writes=()):
        o = Op()
        o.engine, o.fn, o.idx = queue, fn, len(self.ops[queue])
        o.signal, o.semval, o.is_dma = False, None, True
        dsem.count += 16
        o.dmasem, o.dmacount = dsem, dsem.count
        deps = self._collect(reads, writes)
        o.waits = self._filter(o, deps, dma=True)
        self.ops[queue].append(o)
        me = ("dma", dsem, dsem.count)
        key = "dma_" + dsem.name
        for t in reads:
            t.r[key] = me
        for t in writes:
            t.w = {key: me}
            t.r = {}
        return o

    def _filter(self, o, deps, dma=False):
        out = []
        seen = set()
        for d in deps:
            if d[0] == "eng":
                _, eng, idx = d
                if eng == o.engine and not dma:
                    if eng == "pe":
                        continue
                k = (eng, idx)
            else:
                k = (id(d[1]), d[2])
            if k in seen:
                continue
            seen.add(k)
            out.append(d)
            if d[0] == "eng":
                self.ops[d[1]][d[2]].signal = True
        return out

    def emit(self):
        nc = self.nc
        for e in ("pe", "act", "dve", "pool"):
            c = 0
            for o in self.ops[e]:
                if o.signal and not o.is_dma:
                    c += 1
                    o.semval = c
        sched = self

        def run(engname):
            def body(e):
                seen = {}
                for o in sched.ops[engname]:
                    best = {}
                    for d in o.waits:
                        if d[0] == "eng":
                            sem = sched.sems[d[1]]
                            val = sched.ops[d[1]][d[2]].semval
                            key = "e" + d[1]
                        else:
                            sem = d[1].sem
                            val = d[2]
                            key = "d" + d[1].name
                        if seen.get(key, 0) >= val:
                            continue
                        if key not in best or best[key][1] < val:
                            best[key] = (sem, val)
                    for key, (sem, val) in best.items():
                        e.wait_ge(sem, val)
                        seen[key] = val
                    ins = o.fn(e)
                    if o.is_dma:
                        ins.then_inc(o.dmasem.sem, 16)
                    elif o.signal:
                        ins.then_inc(sched.sems[engname], 1)
            return body

        with nc.Block() as block:
            block.tensor(run("pe"))
            block.scalar(run("act"))
            block.vector(run("dve"))
            block.gpsimd(run("pool"))
            block.sync(run("sp"))


class Arena:
    def __init__(self, tensor, ncols_bf16):
        self.t = tensor
        self.n = ncols_bf16
        self.tiles = []

    def tile(self, name, lo_b, nbytes, gen):
        hi_b = lo_b + nbytes
        assert hi_b <= self.n * 2, (name, lo_b, nbytes, self.n * 2)
        t = Tile(name, lo=lo_b, hi=hi_b)
        t.inherit = _LazyInherit(self, t, gen)
        self.tiles.append((t, gen))
        return t

    def ap(self, lo_b, nbytes, dtype):
        a = self.t[:, lo_b // 2:(lo_b + nbytes) // 2]
        if dtype != BF16:
            a = a.bitcast(dtype)
        return a


class _LazyInherit:
    def __init__(self, arena, tile, gen):
        self.arena, self.tile, self.gen = arena, tile, gen

    def __iter__(self):
        me = self.tile
        for t, g in self.arena.tiles:
            if g < self.gen and t.lo < me.hi and me.lo < t.hi:
                for d in t.w.values():
                    yield d
                for d in t.r.values():
                    yield d
                for d in t.inherit:
                    yield d


def _kblk(w):
    n = w.shape[1]
    return np.ascontiguousarray(w.reshape(8, 128, n).transpose(1, 0, 2))


def prep_shared(inp):
    w_in = np.asarray(inp["w_in"], np.float32)[0]
    cs = np.cumsum((0,) + IN_SPLITS)
    gq, gk, gv, gg, lrf, lrb, cq, ckv, kr, mg = [w_in[:, cs[i]:cs[i + 1]] for i in range(10)]
    krrot = np.concatenate([kr[:, 32:64], kr[:, 0:32]], axis=1)
    w_gla = _kblk(np.concatenate([lrf, lrb, gq, gk, gv, gg], axis=1))
    w_mla = _kblk(np.concatenate([cq, ckv, kr, kr, krrot, krrot, mg], axis=1))
    w_uq = np.asarray(inp["mla_w_uq"], np.float32)[0]
    nope = [w_uq[:, h * 192:h * 192 + 128] for h in range(4)]
    rope = [w_uq[:, h * 192 + 128:h * 192 + 192] for h in range(4)]
    rot = [np.concatenate([r[:, 32:64], r[:, 0:32]], axis=1) for r in rope]
    uq_all = np.concatenate(nope + rope + rot, axis=1)
    w_uq_arr = np.ascontiguousarray(uq_all.reshape(2, 128, 1024).transpose(1, 0, 2))
    w_ukv = np.asarray(inp["mla_w_ukv"], np.float32)[0]
    kcols = [w_ukv[:, h * 256:h * 256 + 128] for h in range(4)]
    vcols = [w_ukv[:, h * 256 + 128:h * 256 + 256] for h in range(4)]
    w_ukv_arr = np.ascontiguousarray(np.concatenate(kcols + vcols, axis=1))
    wg2pad = np.zeros((32, 512), np.float32)
    wg2pad[0:16, 0:256] = np.asarray(inp["gla_wg2_fwd"], np.float32)[0]
    wg2pad[16:32, 256:512] = np.asarray(inp["gla_wg2_bwd"], np.float32)[0]
    vecs = np.zeros((128, 16), np.float32)
    bgf = np.asarray(inp["gla_bg_fwd"], np.float32)[0]
    bgb = np.asarray(inp["gla_bg_bwd"], np.float32)[0]
    vecs[:, 0] = bgf[0:128]
    vecs[:, 1] = bgf[128:256]
    vecs[:, 2] = bgb[0:128]
    vecs[:, 3] = bgb[128:256]
    vecs[:, 4] = np.asarray(inp["gla_norm_g"], np.float32)[0]
    qn = np.asarray(inp["mla_q_norm_g"], np.float32)[0]
    vecs[:, 5] = qn[0:128]
    vecs[:, 6] = qn[128:256]
    vecs[:, 7] = np.asarray(inp["mla_kv_norm_g"], np.float32)[0]
    inv_freq = (10000.0 ** (-np.arange(32, dtype=np.float32) / np.float32(32))).astype(np.float32)
    vecs[:, 8] = np.tile(inv_freq, 4)
    w_out = _kblk(np.asarray(inp["w_out"], np.float32)[0])
    lngb = np.stack([np.asarray(inp["ln_g"], np.float32)[0], np.asarray(inp["ln_b"], np.float32)[0]])
    return dict(w_gla=w_gla, w_mla=w_mla, w_uq=w_uq_arr, w_ukv=w_ukv_arr, wg2pad=wg2pad,
                vecs=vecs, w_out=w_out, lngb=np.ascontiguousarray(lngb))


class PsumPool:
    def __init__(self, nc, es):
        self.all = es.enter_context(nc.psum_tensor("pall", [128, 4096], F32))
        self.t = [Tile(f"pb{i}") for i in range(8)]
        self.i = 0
        self.skip = set()

    def next(self):
        while True:
            k = self.i
            self.i = (self.i + 1) % 8
            if k not in self.skip:
                return k


TWO_PI = 2.0 * math.pi
NXS = 4


def build_program(stop_after=99, dumps=()):
    nc = bass.Bass("TRN2", target_bir_lowering=False)
    dram = lambda name, shape, dt, kind="ExternalInput": nc.dram_tensor(name, list(shape), dt, kind=kind).ap()
    x_d = dram("x", [SEQ, D_MODEL], F32)
    pos_d = dram("pos", [1, SEQ], I32)
    w_gla_d = dram("w_gla", [128, 8, 1568], F32)
    w_mla_d = dram("w_mla", [128, 8, 1152], F32)
    w_uq_d = dram("w_uq", [128, 2, 1024], F32)
    w_ukv_d = dram("w_ukv", [128, 1024], F32)
    wg2_d = dram("wg2pad", [32, 512], F32)
    vecs_d = dram("vecs", [128, 16], F32)
    w_out_d = dram("w_out", [128, 8, 1024], F32)
    lngb_d = dram("lngb", [2, D_MODEL], F32)
    out_d = dram("out", [SEQ, D_MODEL], F32, kind="ExternalOutput")
    dump_d = {}

    with ExitStack() as es:
        S = Sched(nc, es)
        sb = lambda name, shape, dt: es.enter_context(nc.sbuf_tensor("s_" + name, list(shape), dt))
        PS = PsumPool(nc, es)
        pb = lambda k: PS.all[:, k * 512:(k + 1) * 512]
        pbb = lambda k: PS.all[:, k * 512:(k + 1) * 512].bitcast(BF16)
        pb2 = lambda k: PS.all[:, k * 512:(k + 2) * 512]

        def MM(k, out, pairs, reads):
            pairs = list(pairs)

            def f(e):
                n = len(pairs)
                for i, (l, r) in enumerate(pairs):
                    ins = e.matmul(out, lhsT=l, rhs=r, start=(i == 0), stop=(i == n - 1))
                return ins
            S.op("pe", f, reads=reads, writes=[PS.t[k]])

        def MMS(items, reads, writes):
            items = list(items)

            def f(e):
                for it in items:
                    out, l, r, st, sp = it[:5]
                    kw = {}
                    if len(it) > 5 and it[5] is not None:
                        kw["tile_position"] = it[5]
                    if len(it) > 6 and it[6]:
                        kw["skip_group_check"] = True
                    ins = e.matmul(out, lhsT=l, rhs=r, start=st, stop=sp, **kw)
                return ins
            S.op("pe", f, reads=reads, writes=writes)

        def TR(k, items, reads):
            items = list(items)

            def f(e):
                for (out, in_) in items:
                    ins = e.transpose(out=out, in_=in_, identity=ident[:])
                return ins
            S.op("pe", f, reads=list(reads) + [t_const], writes=[PS.t[k]])

        def ACT(out, in_, func, reads, writes, scale=None, bias=None):
            kw = {}
            if scale is not None:
                kw["scale"] = scale
            if bias is not None:
                kw["bias"] = bias
            S.op("act", lambda e: e.activation(out=out, in_=in_, func=func, **kw), reads=reads, writes=writes)

        def COPY(eng, out, in_, reads, writes):
            if eng == "act":
                S.op("act", lambda e: e.copy(out=out, in_=in_), reads=reads, writes=writes)
            else:
                S.op(eng, lambda e: e.tensor_copy(out=out, in_=in_), reads=reads, writes=writes)

        def TT(eng, out, in0, in1, op, reads, writes):
            S.op(eng, lambda e: e.tensor_tensor(out=out, in0=in0, in1=in1, op=op), reads=reads, writes=writes)

        def TS(eng, out, in0, s1, s2, op0, op1, reads, writes):
            if op1 is None:
                S.op(eng, lambda e: e.tensor_scalar(out=out, in0=in0, scalar1=s1, scalar2=None, op0=op0), reads=reads, writes=writes)
            else:
                S.op(eng, lambda e: e.tensor_scalar(out=out, in0=in0, scalar1=s1, scalar2=s2, op0=op0, op1=op1), reads=reads, writes=writes)

        def STT(out, in0, scalar, in1, op0, op1, reads, writes):
            S.op("dve", lambda e: e.scalar_tensor_tensor(out=out, in0=in0, scalar=scalar, in1=in1, op0=op0, op1=op1),
                 reads=reads, writes=writes)

        dma_ctr = [0]

        def DMA(queue, out, in_, dsem, reads=(), writes=()):
            if dsem is None:
                dma_ctr[0] += 1
                dsem = S.dmasem(f"o{dma_ctr[0]}")
            S.dma(queue, lambda e: e.dma_start(out=out, in_=in_), dsem, reads=reads, writes=writes)

        ident = sb("ident", [128, 128], BF16)
        identf = sb("identf", [128, 128], F32)
        ones = sb("ones", [128, 384], BF16)
        sel = sb("sel", [128, 128], F32)
        selb = sb("selb", [128, 128], BF16)
        mask01 = sb("mask01", [128, 512], F32)
        maskFB = sb("maskFB", [128, 256], F32)
        vecs = sb("vecs", [128, 16], F32)
        negbg = sb("negbg", [128, 4], F32)
        decF = sb("decF", [128, 16, 2], F32)
        decB = sb("decB", [128, 16, 2], F32)
        decS = sb("decS", [128, 16, 4], F32)
        tmp4 = sb("tmp4", [128, 8], F32)
        lnst = sb("lnst", [128, 4, 16], F32)
        wg2 = sb("wg2", [32, 512], BF16)
        wuq = sb("wuq", [128, 2, 1024], BF16)
        wukv = sb("wukv", [128, 1024], BF16)
        cs2 = sb("cs2", [128, 2, SEQ], BF16)
        xs = [sb(f"xs{i}", [128, D_MODEL], BF16) for i in range(NXS)]
        gateg = sb("gateg", [128, 4, SEQ], BF16)
        XTA = Arena(sb("xta", [128, 16384], BF16), 16384)
        WAR = Arena(sb("war", [128, 8 * 1568], BF16), 8 * 1568)
        ARE = Arena(sb("arena", [128, 43008], BF16), 43008)
        SCR = Arena(sb("scr", [128, 11360], BF16), 11360)

        t_const = Tile("const")
        t_vecs = Tile("vecs")
        t_wg2 = Tile("wg2")
        t_wuq = Tile("wuq")
        t_wukv = Tile("wukv")
        t_cs2 = []
        t_dec = Tile("dec")
        xs_t = [Tile(f"xs{i}") for i in range(NXS)]
        gateg_t = [Tile(f"gateg{g}") for g in range(NG)]

        xT = XTA.ap(0, 32768, BF16).rearrange("p (k t) -> p k t", k=8)
        xT_t = [XTA.tile(f"xT{tt}", 0, 32768, 0) for tt in range(NT)]
        wgla = WAR.ap(0, 8 * 1568 * 2, BF16).rearrange("p (k c) -> p k c", k=8)
        wgla_t = {n: WAR.tile("wgla_" + n, 0, 8 * 1568 * 2, 1) for n in ("lr", "qk", "v", "g")}

        sem_xs = [S.dmasem(f"xs{i}") for i in range(NXS)]
        sem_dump = S.dmasem("dump")
        dump_tiles = []

        def dump(name, ap, tiles, shape, dt):
            if name not in dumps:
                return
            d = nc.dram_tensor("dump_" + name, list(shape), dt, kind="ExternalOutput").ap()
            dump_d[name] = d
            t = Tile("dump_" + name)
            dump_tiles.append(t)
            DMA("sp", d, ap, sem_dump, reads=list(tiles), writes=[t])

        def load_x(tt):
            sl = tt % NXS
            DMA("pool", xs[sl][:], x_d[tt * 128:(tt + 1) * 128, :], sem_xs[sl], writes=[xs_t[sl]])
        NF = 4
        xf = [ARE.ap(68608 + i * 4096, 4096, F32) for i in range(NF)]
        xf_t = [ARE.tile(f"xf{i}", 68608 + i * 4096, 4096, 0) for i in range(NF)]
        DMA("sp", vecs[:], vecs_d, None, writes=[t_vecs])
        wpieces = [("lr", 0, 32), ("v", 544, 800), ("v", 800, 1056), ("qk", 32, 288), ("qk", 288, 544)]
        wst = []

        def wstage(pi_):
            n_, a_, b_ = wpieces[pi_]
            lo_ = 0 if pi_ == 0 else 1024 + (pi_ - 1) * 8192
            nb_ = 1024 if pi_ == 0 else 8192
            st_ap = ARE.ap(lo_, nb_, F32).rearrange("p (k c) -> p k c", k=8)
            st_t = ARE.tile(f"wst{pi_}", lo_, nb_, 0)
            wst.append((st_ap, st_t))
            DMA("sp", st_ap, w_gla_d[:, :, a_:b_], None, writes=[st_t])
        wstage(0)
        for i in range(NF):
            DMA("sp", xf[i], x_d[i * 128:(i + 1) * 128, :], None, writes=[xf_t[i]])
        DMA("pool", wg2[:], wg2_d, None, writes=[t_wg2])
        for pi_ in range(1, len(wpieces)):
            wstage(pi_)

        def wcast(pi_):
            n_, a_, b_ = wpieces[pi_]
            COPY("act", wgla[:, :, a_:b_], wst[pi_][0], [wst[pi_][1]], [wgla_t[n_]])

        t_identf = Tile("identf")
        t_m01 = Tile("m01")
        t_mfb = Tile("mfb")
        S.op("pool", lambda e: e.memset(identf[:], 0.0), writes=[t_identf])
        S.op("pool", lambda e: e.affine_select(out=identf[:], in_=identf[:], pattern=[[-1, 128]], compare_op=ALU.not_equal,
                                               fill=1.0, base=0, channel_multiplier=1), reads=[t_identf], writes=[t_identf])
        S.op("pool", lambda e: e.tensor_copy(out=ident[:], in_=identf[:]), reads=[t_identf], writes=[t_const])
        S.op("pool", lambda e: e.memset(ones[:, 0:128], 1.0 / 256.0), writes=[t_const])
        S.op("pool", lambda e: e.memset(ones[:, 128:256], 1.0 / 128.0), writes=[t_const])
        S.op("pool", lambda e: e.memset(ones[:, 256:384], 1.0), writes=[t_const])
        t_sel = Tile("sel")
        S.op("pool", lambda e: e.memset(sel[:], 0.0), writes=[t_sel])
        for p0 in (0, 32, 64, 96):
            S.op("pool", lambda e, p0=p0: e.affine_select(out=sel[:], in_=sel[:], pattern=[[0, 128]], compare_op=ALU.not_equal,
                                                          fill=1.0, base=-p0, channel_multiplier=1), reads=[t_sel], writes=[t_sel])
        S.op("pool", lambda e: e.tensor_copy(out=selb[:], in_=sel[:]), reads=[t_sel], writes=[t_sel])
        S.op("pool", lambda e: e.memset(mask01[:], 1.0), writes=[t_m01])
        for c in range(4):
            S.op("pool", lambda e, c=c: e.memset(mask01[:, c * 128:c * 128 + 1], 0.0), reads=[t_m01], writes=[t_m01])
        S.op("pool", lambda e: e.memset(maskFB[:], 1.0), writes=[t_mfb])
        S.op("pool", lambda e: e.affine_select(out=maskFB[:, 0:128], in_=maskFB[:, 0:128], pattern=[[1, 128]],
                                               compare_op=ALU.is_ge, fill=0.0, base=0, channel_multiplier=-1), reads=[t_mfb], writes=[t_mfb])
        S.op("pool", lambda e: e.affine_select(out=maskFB[:, 128:256], in_=maskFB[:, 128:256], pattern=[[-1, 128]],
                                               compare_op=ALU.is_gt, fill=0.0, base=0, channel_multiplier=1), reads=[t_mfb], writes=[t_mfb])
        S.op("pool", lambda e: e.memset(tmp4[:, 0:1], 0.0), reads=[t_m01, t_mfb], writes=[t_const])

        TS("dve", negbg[:], vecs[:, 0:4], -1.0, None, ALU.mult, None, [t_vecs], [t_const])

        def transpose_x(tt):
            if tt < NF:
                k = 2 * tt
                items = [(pb2(k)[:, kc * 128:(kc + 1) * 128], xf[tt][:, kc * 128:(kc + 1) * 128]) for kc in range(8)]

                def f(e):
                    for (o, i_) in items:
                        ins = e.transpose(out=o, in_=i_, identity=identf[:])
                    return ins
                S.op("pe", f, reads=[xf_t[tt], t_identf], writes=[PS.t[k], PS.t[k + 1]])
                for half in range(2):
                    COPY("act" if half == 0 else "dve", xT[:, 4 * half:4 * half + 4, tt * 128:(tt + 1) * 128],
                         pb(k + half).rearrange("p (k t) -> p k t", k=4), [PS.t[k + half]], [xT_t[tt]])
                return
            sl = tt % NXS
            k = PS.next()
            TR(k, [(pbb(k)[:, kc * 128:(kc + 1) * 128], xs[sl][:, kc * 128:(kc + 1) * 128]) for kc in range(8)], [xs_t[sl]])
            COPY("act" if (tt % 2 == 0 or tt < 4) else "dve", xT[:, :, tt * 128:(tt + 1) * 128],
                 pbb(k).rearrange("p (k t) -> p k t", k=8), [PS.t[k]], [xT_t[tt]])

        wsl = {"lr": (0, 32), "qk": (32, 544), "v": (544, 1056), "g": (1056, 1568)}

        def load_wgla(n):
            a, b = wsl[n]
            DMA("pool", wgla[:, :, a:b], w_gla_d[:, :, a:b], None, writes=[wgla_t[n]])

        dump("xT", xT, xT_t, [128, 8, SEQ], BF16)
        dump("cs2", cs2[:], t_cs2, [128, 2, SEQ], BF16)


        QD = {}
        QD_t = {}
        for i, n in enumerate(("qdf", "qdb", "kdf", "kdb")):
            QD[n] = ARE.ap(i * 8192, 8192, BF16).rearrange("p (j t) -> p j t", j=2)
            QD_t[n] = [ARE.tile(f"{n}{g}", i * 8192, 8192, 1) for g in range(NG)]
        VG = ARE.ap(32768, 16384, BF16).rearrange("p (c v) -> p c v", c=NT)
        VG_t = [ARE.tile(f"vg{tt}", 32768, 16384, 1) for tt in range(NT)]
        SST = ARE.ap(49152, 16384, BF16).rearrange("p (d c v) -> p d c v", d=2, c=NT)
        SST_t = [[ARE.tile(f"sst{d}_{c}", 49152, 16384, 1) for c in range(NT)] for d in range(2)]
        KDT = [(ARE.ap(65536 + i * 1024, 1024, BF16), ARE.tile(f"kdt{i}", 65536 + i * 1024, 1024, 1)) for i in range(3)]

        scr_off = [0]

        def salloc(name, nbytes, dt, gen):
            lo = scr_off[0]
            scr_off[0] += nbytes
            assert scr_off[0] <= 22720, (name, scr_off[0])
            return SCR.ap(lo, nbytes, dt), SCR.tile(name, lo, nbytes, gen)

        lrT = [salloc(f"lrT{i}", 1024, BF16, 1) for i in range(2)]
        spf = [salloc(f"spf{i}", 2048, F32, 1) for i in range(2)]
        spb = [salloc(f"spb{i}", 2112, F32, 1) for i in range(2)]
        CF = spf
        CB = [salloc(f"CB{i}", 2048, F32, 1) for i in range(2)]
        ET = [salloc(f"E{i}", 2048, F32, 1) for i in range(4)]
        for sp_ap, sp_t in spb:
            S.op("pool", lambda e, sp_ap=sp_ap: e.memset(sp_ap, 0.0), writes=[sp_t])

        LN8 = math.log(0.125)
        ring = [0]

        def phase1_group(g):
            gs = slice(g * 512, (g + 1) * 512)
            xg_t = [xT_t[4 * g + i] for i in range(4)]
            k = PS.next()
            MM(k, pb(k)[0:32, :], [(wgla[:, kc, 0:32], xT[:, kc, gs]) for kc in range(8)], xg_t + [wgla_t["lr"]])
            lr_ap, lr_t = lrT[g % 2]
            COPY("dve" if g == 0 else "act", lr_ap[0:32, :], pb(k)[0:32, :], [PS.t[k]], [lr_t])
            for i in range(4):
                tt = 4 * g + i
                k = PS.next()
                MM(k, pb(k), [(xT[:, kc, tt * 128:(tt + 1) * 128], wgla[:, kc, 544:1056]) for kc in range(8)], [xT_t[tt], wgla_t["v"]])
                COPY("dve" if (i % 2 == 0 or g == 0) else "act", VG[:, tt, :], pb(k), [PS.t[k]], [VG_t[tt]])
            for j in range(2):
                r = ring[0] % 2
                ring[0] += 1
                spf_ap, spf_t = spf[r]
                spb_ap, spb_t = spb[r]
                CF_ap, CF_t = CF[r]
                CB_ap, CB_t = CB[r]
                for d in range(2):
                    k = PS.next()
                    MM(k, pb(k), [(wg2[0:32, d * 256 + j * 128:d * 256 + (j + 1) * 128], lr_ap[0:32, :])], [lr_t, t_wg2])
                    dst = spf_ap if d == 0 else spb_ap[:, 1:513]
                    dt_ = spf_t if d == 0 else spb_t
                    ACT(dst, pb(k), AF.Exp, [PS.t[k], t_const], [dt_], scale=-1.0, bias=negbg[:, 2 * d + j:2 * d + j + 1])
                    ACT(dst, dst, AF.Ln, [dt_], [dt_], bias=1.0)
                S.op("dve", lambda e, CF_ap=CF_ap, spf_ap=spf_ap: e.tensor_tensor_scan(
                    out=CF_ap, data0=mask01[:], data1=spf_ap, initial=0.0, op0=ALU.mult, op1=ALU.add),
                    reads=[spf_t, t_const], writes=[CF_t])
                S.op("dve", lambda e, CB_ap=CB_ap, spb_ap=spb_ap: e.tensor_tensor_scan(
                    out=CB_ap, data0=spb_ap[:, 0:512], data1=mask01[:], initial=0.0, op0=ALU.add, op1=ALU.mult),
                    reads=[spb_t, t_const], writes=[CB_t])
                ACT(decS[:, 4 * g:4 * g + 4, j], CF_ap[:, 127:512:128], AF.Exp, [CF_t], [t_dec], scale=-1.0 / 16.0)
                TT("dve", tmp4[:, 4 * j:4 * j + 4], CB_ap[:, 127:512:128], spb_ap[:, 128:513:128], ALU.add, [CB_t, spb_t], [t_dec])
                for ci in range(4):
                    c = 4 * g + ci
                    if c <= NT - 2:
                        ACT(decS[:, NT - 2 - c, 2 + j:3 + j], tmp4[:, 4 * j + ci:4 * j + ci + 1], AF.Exp, [t_dec], [t_dec], scale=-1.0 / 16.0)
                especs = [(CF_ap, CF_t, -1.0 / 16.0, LN8), (CF_ap, CF_t, 1.0 / 16.0, None),
                          (CB_ap, CB_t, 1.0 / 16.0, LN8), (CB_ap, CB_t, -1.0 / 16.0, None)]
                for i, (src, st, sc, bi) in enumerate(especs):
                    ACT(ET[i][0], src, AF.Exp, [st], [ET[i][1]], scale=sc, bias=bi)
                for qk in range(2):
                    k = PS.next()
                    c0 = 32 + qk * 256 + j * 128
                    MM(k, pb(k), [(wgla[:, kc, c0:c0 + 128], xT[:, kc, gs]) for kc in range(8)], xg_t + [wgla_t["qk"]])
                    for d in range(2):
                        n = ("qd", "kd")[qk] + ("f", "b")[d]
                        ei = d * 2 + qk
                        TT("dve", QD[n][:, j, gs], pb(k), ET[ei][0], ALU.mult, [PS.t[k], ET[ei][1]], [QD_t[n][g]])


            for h in range(4):
                k = PS.next()
                MM(k, pb(k), [(wgla[:, kc, 1056 + h * 128:1056 + (h + 1) * 128], xT[:, kc, gs]) for kc in range(8)], xg_t + [wgla_t["g"]])
                ACT(gateg[:, h, gs], pb(k), AF.Silu, [PS.t[k]], [gateg_t[g]])


        def tr_group(g):
            for i in range(4):
                tt = 4 * g + i
                transpose_x(tt)
                if tt >= NF and tt + NXS < NT:
                    load_x(tt + NXS)

        tr_group(0)
        for tt in range(NF, NF + NXS):
            sl_ = tt % NXS
            DMA("pool", xs[sl_][:], x_d[tt * 128:(tt + 1) * 128, :], sem_xs[sl_], reads=[xT_t[0]], writes=[xs_t[sl_]])
        DMA("pool", wgla[:, :, 1056:1568], w_gla_d[:, :, 1056:1568], None, reads=[xT_t[2]], writes=[wgla_t["g"]])
        for pi_ in range(len(wpieces)):
            wcast(pi_)
        phase1_group(0)
        tr_group(1)
        rA = ARE.ap(50176, 2048, F32)
        rB = ARE.ap(52224, 2048, F32)
        rC = ARE.ap(54272, 2048, I32)
        rD = ARE.ap(56320, 2048, F32)
        rP = ARE.ap(58368, 2048, I32)
        rT = ARE.ap(60416, 3072, BF16).rearrange("p (w t) -> p w t", w=3)
        t_r = {n: ARE.tile("rope_" + n, lo, 3072 if n == "T" else 2048, 0) for n, lo in (("A", 50176), ("B", 52224), ("C", 54272), ("D", 56320), ("P", 58368), ("T", 60416))}
        SIN_S = TWO_PI * (1.0 - 2e-7)
        SIN_B = -math.pi * (1.0 - 2e-7)
        sem_rp = S.dmasem("ropepos")
        for blk in range(4):
            S.dma("sp", lambda e, blk=blk: e.dma_start(out=rP[blk * 32:(blk + 1) * 32, :],
                                                       in_=pos_d[0:1, blk * 512:(blk + 1) * 512].partition_broadcast(32)),
                  sem_rp, writes=[t_r["P"]])
        COPY("dve", rA, rP, [t_r["P"]], [t_r["A"]])
        TS("dve", rA, rA, vecs[:, 8:9], None, ALU.mult, None, [t_r["A"], t_vecs], [t_r["A"]])
        for which, off in ((0, 0.75), (1, 0.5)):
            TS("dve", rB, rA, 1.0 / TWO_PI, off, ALU.mult, ALU.add, [t_r["A"]], [t_r["B"]])
            COPY("dve", rC, rB, [t_r["B"]], [t_r["C"]])
            COPY("dve", rD, rC, [t_r["C"]], [t_r["D"]])
            TT("dve", rB, rB, rD, ALU.subtract, [t_r["B"], t_r["D"]], [t_r["B"]])
            STT(rD, rB, 0.0, rB, ALU.is_lt, ALU.add, [t_r["B"]], [t_r["D"]])
            ACT(rT[:, which, :], rD, AF.Sin, [t_r["D"]], [t_r["T"]], scale=SIN_S, bias=SIN_B)
            if which == 1:
                ACT(rT[:, 2, :], rD, AF.Sin, [t_r["D"]], [t_r["T"]], scale=-SIN_S, bias=-SIN_B)
        for blk in range(4):
            for rep in range(4):
                tcs = Tile(f"cs2_{blk}_{rep}")
                t_cs2.append(tcs)
                srcv = rT[blk * 32:(blk + 1) * 32, 0:3:2, :] if rep % 2 == 0 else rT[blk * 32:(blk + 1) * 32, 0:2, :]
                DMA("sp", cs2[rep * 32:(rep + 1) * 32, :, blk * 512:(blk + 1) * 512], srcv, None,
                    reads=[t_r["T"]], writes=[tcs])

        for g in range(1, NG):
            if g + 1 < NG:
                tr_group(g + 1)
            phase1_group(g)
        DMA("pool", wuq[:], w_uq_d, None, writes=[t_wuq])
        DMA("pool", wukv[:], w_ukv_d, None, writes=[t_wukv])

        for n in ("qdf", "qdb", "kdf", "kdb"):
            dump(n, QD[n], QD_t[n], [128, 2, SEQ], BF16)
        dump("vg", VG, VG_t, [128, NT, 512], BF16)
        dump("gateg", gateg[:], gateg_t, [128, 4, SEQ], BF16)

        if stop_after >= 3:
            wmla = WAR.ap(0, 8 * 1152 * 2, BF16).rearrange("p (k c) -> p k c", k=8)
            wmla_t = {n: WAR.tile("wmla_" + n, 0, 8 * 1152 * 2, 3) for n in ("a", "g")}
            DMA("pool", wmla[:, :, 0:640], w_mla_d[:, :, 0:640], None, writes=[wmla_t["a"]])
            DMA("pool", wmla[:, :, 640:1152], w_mla_d[:, :, 640:1152], None, writes=[wmla_t["g"]])

        if stop_after >= 2:
            scr_off[0] = 0
            Xst, Xst_t = salloc("Xst", 2048, F32, 2)
            X2, X2_t = salloc("X2", 2048, F32, 2)
            ABT = [salloc(f"ABT{i}", 2048, BF16, 2) for i in range(2)]
            SQ = [salloc(f"sq{i}", 1024, BF16, 2) for i in range(2)]
            RS = [salloc(f"rs{i}", 2048, F32, 2) for i in range(3)]
            TQ = [salloc(f"tq{i}", 2048, F32, 2) for i in range(3)]
            S.op("dve", lambda e: e.memset(Xst, 0.0), writes=[Xst_t])
            S.op("pool", lambda e: e.memset(SST[:, 0, 0, :], 0.0), writes=[SST_t[0][0]])
            S.op("pool", lambda e: e.memset(SST[:, 1, NT - 1, :], 0.0), writes=[SST_t[1][NT - 1]])
            NS = NT - 1
            tr_k = {}
            t_decS = t_dec

            def chain_tr(s):
                cf, cb = s, NT - 1 - s
                kd_ap, kd_t = KDT[s % 3]
                k = PS.next()
                tr_k[s] = k
                items = []
                for d, c, n in ((0, cf, "kdf"), (1, cb, "kdb")):
                    for j in range(2):
                        q = d * 2 + j
                        items.append((pbb(k)[:, q * 128:(q + 1) * 128], QD[n][:, j, c * 128:(c + 1) * 128]))
                TR(k, items, [QD_t["kdf"][cf // 4], QD_t["kdb"][cb // 4]])
                COPY("act", kd_ap, pbb(k)[:, 0:512], [PS.t[k]], [kd_t])

            def chain_u(s):
                cf, cb = s, NT - 1 - s
                kd_ap, kd_t = KDT[s % 3]
                k2 = PS.next()
                items = []
                for d, c in ((0, cf), (1, cb)):
                    for j in range(2):
                        q = d * 2 + j
                        for i in range(2):
                            items.append((pb(k2)[i * 64:(i + 1) * 64, q * 128:(q + 1) * 128],
                                          kd_ap[:, q * 128 + i * 64:q * 128 + (i + 1) * 64],
                                          VG[:, c, (2 * j + i) * 128:(2 * j + i + 1) * 128], True, True))
                MMS(items, [kd_t, VG_t[cf], VG_t[cb]], [PS.t[k2]])
                TT("dve", X2, pb(k2), Xst, ALU.add, [PS.t[k2], Xst_t], [X2_t])
                TT("dve", Xst.rearrange("p (q v) -> p q v", q=4), X2.rearrange("p (q v) -> p q v", q=4),
                   decS[:, s, :].unsqueeze(2).broadcast_to([128, 4, 128]), ALU.mult, [X2_t, t_decS], [Xst_t])
                COPY("act", SST[:, 0, s + 1, :], Xst[:, 0:256], [Xst_t], [SST_t[0][s + 1]])
                COPY("act", SST[:, 1, NT - 2 - s, :], Xst[:, 256:512], [Xst_t], [SST_t[1][NT - 2 - s]])

            chain_tr(0)
            chain_tr(1)
            for s in range(NS):
                if s + 2 < NS:
                    chain_tr(s + 2)
                chain_u(s)
            dump("sst", SST, SST_t[0] + SST_t[1], [128, 2, NT, 256], BF16)

        if stop_after >= 2.5:
            hv = lambda ap: ap.rearrange("p (h t) -> p h t", h=4)
            abank = [0, 1]
            obank = [(2, 3), (4, 5)]
            mbanks = [6, 7]
            PS.i = 0
            mask4 = maskFB[:].rearrange("p (d t) -> p d t", d=2).unsqueeze(2).broadcast_to([128, 2, 2, 128])

            def stage_A(c):
                cs_ = slice(c * 128, (c + 1) * 128)
                g = c // 4
                ka, kb_ = abank
                items = []
                for di, (qn, kn) in enumerate((("qdf", "kdf"), ("qdb", "kdb"))):
                    for j in range(2):
                        for i, kk in ((0, ka), (1, kb_)):
                            col = (di * 2 + j) * 128
                            items.append((pb(kk)[:, col:col + 128], QD[kn][i * 64:(i + 1) * 64, j, cs_],
                                          QD[qn][i * 64:(i + 1) * 64, j, cs_], True, True))
                MMS(items, [QD_t[n][g] for n in ("qdf", "kdf", "qdb", "kdb")], [PS.t[ka], PS.t[kb_]])
                A_ = ABT[c % 2]
                for i, kk in ((0, ka), (1, kb_)):
                    outv = A_[0].rearrange("p (d j i t) -> p d j i t", d=2, j=2, i=2)[:, :, :, i, :]
                    TT("dve", outv, pb(kk).rearrange("p (d j t) -> p d j t", d=2, j=2), mask4, ALU.mult,
                       [PS.t[kk], t_const], [A_[1]])

            def stage_B(c):
                cs_ = slice(c * 128, (c + 1) * 128)
                g = c // 4
                kOe, kOo = obank[c % 2]
                abt = ABT[c % 2]
                abv = abt[0].rearrange("p (d h t) -> p d h t", d=2, h=4)
                items = []
                for i, kO in ((0, kOe), (1, kOo)):
                    for j in range(2):
                        h = 2 * j + i
                        o = pb(kO)[:, j * 128:(j + 1) * 128]
                        v = VG[:, c, h * 128:(h + 1) * 128]
                        items.append((o, v, abv[:, 0, h, :], j == 0, False, None, True))
                        items.append((o, v, abv[:, 1, h, :], False, False, None, True))
                for j in range(2):
                    for d, qn in ((0, "qdf"), (1, "qdb")):
                        for i, kO in ((0, kOe), (1, kOo)):
                            o = pb(kO)[:, j * 128:(j + 1) * 128]
                            items.append((o, SST[i * 64:(i + 1) * 64, d, c, j * 128:(j + 1) * 128],
                                          QD[qn][i * 64:(i + 1) * 64, j, cs_], False, (j == 1 and d == 1), None, True))
                MMS(items, [VG_t[c], abt[1], SST_t[0][c], SST_t[1][c], QD_t["qdf"][g], QD_t["qdb"][g]], [PS.t[kOe], PS.t[kOo]])
                sq = SQ[c % 2]
                tq = TQ[c % 3]
                o2 = pb2(kOe).rearrange("p (b c) -> p b c", b=2)[:, :, 0:256]
                ACT(sq[0].rearrange("p (b c) -> p b c", b=2), o2, AF.Square, [PS.t[kOe], PS.t[kOo]], [sq[1]])

            def stage_B2(c):
                sq = SQ[c % 2]
                rs = RS[c % 3]
                mbank = mbanks[c % 2]
                MM(mbank, pb(mbank), [(ones[:, 128:256], sq[0])], [sq[1], t_const])
                ACT(rs[0], pb(mbank), AF.Ln, [PS.t[mbank]], [rs[1]], bias=RMS_EPS)
                ACT(rs[0], rs[0], AF.Exp, [rs[1]], [rs[1]], scale=-0.5)

            def stage_C(c):
                cs_ = slice(c * 128, (c + 1) * 128)
                g = c // 4
                rs = RS[c % 3]
                tq = TQ[c % 3]
                kOe, kOo = obank[c % 2]
                o2 = pb2(kOe).rearrange("p (b c) -> p b c", b=2)[:, :, 0:256]
                STT(tq[0].rearrange("p (b c) -> p b c", b=2), o2, vecs[:, 4:5], rs[0].rearrange("p (b c) -> p b c", b=2),
                    ALU.mult, ALU.mult, [PS.t[kOe], PS.t[kOo], rs[1], t_vecs], [tq[1]])
                gv = gateg[:, :, cs_].rearrange("p (j i) t -> p i j t", i=2)
                TT("pool", gv, tq[0].rearrange("p (i j t) -> p i j t", i=2, j=2), gv, ALU.mult, [tq[1], gateg_t[g]], [gateg_t[g]])

            for it in range(NT + 3):
                if it < NT:
                    stage_A(it)
                if 0 <= it - 1 < NT:
                    stage_B(it - 1)
                if 0 <= it - 2 < NT:
                    stage_B2(it - 2)
                if 0 <= it - 2 < NT:
                    stage_C(it - 2)
            dump("mixg", gateg[:], gateg_t, [128, 4, SEQ], BF16)

        if stop_after >= 3:
            KT = ARE.ap(0, 16384, BF16).rearrange("p (h t) -> p h t", h=4)
            KT_t = [ARE.tile(f"KT{g}", 0, 16384, 3) for g in range(NG)]
            QT = ARE.ap(16384, 16384, BF16).rearrange("p (h t) -> p h t", h=4)
            QT_t = [ARE.tile(f"QT{g}", 16384, 16384, 3) for g in range(NG)]
            VM = ARE.ap(32768, 16384, BF16).rearrange("p (c v) -> p c v", c=NT)
            VM_t = [ARE.tile(f"VM{tt}", 32768, 16384, 3) for tt in range(NT)]
            GM = ARE.ap(49152, 16384, BF16).rearrange("p (h t) -> p h t", h=4)
            GM_t = [ARE.tile(f"GM{g}", 49152, 16384, 3) for g in range(NG)]
            KR = ARE.ap(65536, 4096, BF16)
            KR_t = [ARE.tile(f"KR{g}", 65536, 4096, 3) for g in range(NG)]
            QR = ARE.ap(69632, 16384, BF16).rearrange("p (h t) -> p h t", h=4)
            QR_t = [ARE.tile(f"QR{g}", 69632, 16384, 3) for g in range(NG)]
            S.op("pool", lambda e: e.memset(ARE.ap(69632, 16384, BF16), 0.0), writes=QR_t)

            scr_off[0] = 0
            SQ3 = [[salloc(f"sq3_{i}_{r}", 1024, BF16, 3) for i in range(3)] for r in range(2)]
            RQ = [salloc("rq", 2048, F32, 3)] * 2
            RKV = [salloc("rkv", 2048, F32, 3)] * 2
            CQN = [salloc(f"cqn{r}", 2048, BF16, 3) for r in range(2)]
            CKVN = [salloc(f"ckvn{r}", 1024, BF16, 3) for r in range(2)]
            TRR = [salloc(f"trr{r}", 2048, F32, 3) for r in range(3)]
            rr = 0
            hold = [0, 1, 2]
            PS.skip = set(hold)

            def rope_combine(kA, kB, outs, out_t, gs):
                nonlocal rr
                t1, t1t = TRR[rr % 3]
                t2, t2t = TRR[(rr + 1) % 3]
                rr += 2
                TT("dve", t1, pb(kA), cs2[:, 0, gs], ALU.mult, [PS.t[kA]] + t_cs2, [t1t])
                TT("dve", t2, pb(kB), cs2[:, 1, gs], ALU.mult, [PS.t[kB]] + t_cs2, [t2t])
                for (p0, p1, out) in outs:
                    TT("pool", out, t1[p0:p1, :], t2[p0:p1, :], ALU.add, [t1t, t2t], [out_t])

            p3 = {}

            def A_units(g):
                gs = slice(g * 512, (g + 1) * 512)
                xg_t = [xT_t[4 * g + i] for i in range(4)]
                r = g % 2
                kk = [hold[bi] for bi in range(3)]
                p3[g] = kk
                U = []

                def u_c(bi):
                    k = kk[bi]
                    MM(k, pb(k), [(wmla[:, kc, bi * 128:(bi + 1) * 128], xT[:, kc, gs]) for kc in range(8)], xg_t + [wmla_t["a"]])
                    ACT(SQ3[r][bi][0], pb(k), AF.Square, [PS.t[k]], [SQ3[r][bi][1]])

                def u_gate(h):
                    k = PS.next()
                    MM(k, pb(k), [(wmla[:, kc, 640 + h * 128:640 + (h + 1) * 128], xT[:, kc, gs]) for kc in range(8)], xg_t + [wmla_t["g"]])
                    ACT(GM[:, h, gs], pb(k), AF.Silu, [PS.t[k]], [GM_t[g]])

                def u_stats():
                    kq = PS.next()
                    MM(kq, pb(kq), [(ones[:, 0:128], SQ3[r][0][0]), (ones[:, 0:128], SQ3[r][1][0])], [SQ3[r][0][1], SQ3[r][1][1], t_const])
                    ACT(RQ[r][0], pb(kq), AF.Ln, [PS.t[kq]], [RQ[r][1]], bias=RMS_EPS)
                    ACT(RQ[r][0], RQ[r][0], AF.Exp, [RQ[r][1]], [RQ[r][1]], scale=-0.5)
                    kv_ = PS.next()
                    MM(kv_, pb(kv_), [(ones[:, 128:256], SQ3[r][2][0])], [SQ3[r][2][1], t_const])
                    ACT(RKV[r][0], pb(kv_), AF.Ln, [PS.t[kv_]], [RKV[r][1]], bias=RMS_EPS)
                    ACT(RKV[r][0], RKV[r][0], AF.Exp, [RKV[r][1]], [RKV[r][1]], scale=-0.5)

                def u_kr():
                    kR = PS.next()
                    MM(kR, pb(kR), [(wmla[:, kc, 384:512], xT[:, kc, gs]) for kc in range(8)], xg_t + [wmla_t["a"]])
                    kS = PS.next()
                    MM(kS, pb(kS), [(wmla[:, kc, 512:640], xT[:, kc, gs]) for kc in range(8)], xg_t + [wmla_t["a"]])
                    rope_combine(kR, kS, [(0, 128, KR[:, gs])], KR_t[g], gs)

                def u_norm():
                    cqn = CQN[r][0].rearrange("p (k t) -> p k t", k=2)
                    for kc in range(2):
                        STT(cqn[:, kc, :], pb(kk[kc]), vecs[:, 5 + kc:6 + kc], RQ[r][0], ALU.mult, ALU.mult,
                            [PS.t[kk[kc]], RQ[r][1], t_vecs], [CQN[r][1]])
                    STT(CKVN[r][0], pb(kk[2]), vecs[:, 7:8], RKV[r][0], ALU.mult, ALU.mult, [PS.t[kk[2]], RKV[r][1], t_vecs], [CKVN[r][1]])

                U += [lambda: u_c(0), lambda: u_c(1), lambda: u_c(2), lambda: u_gate(0), lambda: u_gate(1), u_stats,
                      lambda: u_gate(2), lambda: u_gate(3), u_norm, u_kr]
                return U

            def B_units(g):
                gs = slice(g * 512, (g + 1) * 512)
                r = g % 2
                cqn = CQN[r][0].rearrange("p (k t) -> p k t", k=2)
                U = []

                def u_qn(h):
                    k = PS.next()
                    MM(k, pb(k), [(wuq[:, kc, h * 128:(h + 1) * 128], cqn[:, kc, :]) for kc in range(2)], [CQN[r][1], t_wuq])
                    COPY("act" if h % 2 == 0 else "dve", QT[:, h, gs], pb(k), [PS.t[k]], [QT_t[g]])

                def u_qr(pr):
                    kA = PS.next()
                    MM(kA, pb(kA), [(wuq[:, kc, 512 + pr * 128:512 + (pr + 1) * 128], cqn[:, kc, :]) for kc in range(2)], [CQN[r][1], t_wuq])
                    kB = PS.next()
                    MM(kB, pb(kB), [(wuq[:, kc, 768 + pr * 128:768 + (pr + 1) * 128], cqn[:, kc, :]) for kc in range(2)], [CQN[r][1], t_wuq])
                    rope_combine(kA, kB, [(0, 64, QR[0:64, 2 * pr, gs]), (64, 128, QR[64:128, 2 * pr + 1, gs])], QR_t[g], gs)

                def u_kn(h):
                    k = PS.next()
                    MM(k, pb(k), [(wukv[:, h * 128:(h + 1) * 128], CKVN[r][0])], [CKVN[r][1], t_wukv])
                    COPY("dve" if h % 2 == 0 else "act", KT[:, h, gs], pb(k), [PS.t[k]], [KT_t[g]])

                def u_v(i):
                    tt = 4 * g + i
                    k = PS.next()
                    MM(k, pb(k), [(CKVN[r][0][:, i * 128:(i + 1) * 128], wukv[:, 512:1024])], [CKVN[r][1], t_wukv])
                    COPY("act" if i % 2 == 0 else "dve", VM[:, tt, :], pb(k), [PS.t[k]], [VM_t[tt]])

                for h in range(4):
                    U.append(lambda h=h: u_qn(h))
                    U.append(lambda h=h: u_kn(h))
                    if h % 2 == 1:
                        U.append(lambda pr=h // 2: u_qr(pr))
                    U.append(lambda i=h: u_v(i))
                return U

            for u in A_units(0):
                u()
            for g in range(NG):
                Bu = B_units(g)
                Au = A_units(g + 1) if g + 1 < NG else []
                ia = ib = 0
                while ia < len(Au) or ib < len(Bu):
                    if ia < len(Au):
                        Au[ia]()
                        ia += 1
                    for _ in range(2 if ia < len(Au) else 1):
                        if ib < len(Bu):
                            Bu[ib]()
                            ib += 1
            PS.skip = set()
            dump("KT", KT, KT_t, [128, 4, SEQ], BF16)
            dump("QT", QT, QT_t, [128, 4, SEQ], BF16)
            dump("VM", VM, VM_t, [128, NT, 512], BF16)
            dump("KR", KR, KR_t, [128, SEQ], BF16)
            dump("QR", QR, QR_t, [128, 4, SEQ], BF16)
            dump("GM", GM, GM_t, [128, 4, SEQ], BF16)

        if stop_after >= 4:
            wout = WAR.ap(0, 16384, BF16).rearrange("p (k c) -> p k c", k=8)
            wout_t = WAR.tile("wout", 0, 16384, 5)
            DMA("pool", wout, w_out_d, None, writes=[wout_t])
            scr_off[0] = 0
            NPT = 12
            PT = [salloc(f"pt{i}", 1024, BF16, 4) for i in range(NPT)]
            RD = [salloc(f"rd{i}", 2048, F32, 4) for i in range(2)]
            TQ4 = [salloc(f"tq4{i}", 2048, F32, 4) for i in range(2)]
            D4 = salloc("den4hi", 1024, BF16, 4)
            D4L = salloc("den4lo", 1024, BF16, 4)
            obanks = [PS.next(), PS.next()]
            d4bank = PS.next()
            dbbank = PS.next()
            sbanks = [PS.next() for _ in range(4)]
            blocks = [(h, qg) for h in range(4) for qg in range(NG)]
            pairs_ = [(bi, kp) for bi in range(len(blocks)) for kp in range(NT // 2)]

            def issue_S_pair(pi):
                bi, kp = pairs_[pi]
                h, qg = blocks[bi]
                qs = slice(qg * 512, (qg + 1) * 512)
                b0, b1 = sbanks[(2 * pi) % 4], sbanks[(2 * pi + 1) % 4]
                k0 = slice((2 * kp) * 128, (2 * kp + 1) * 128)
                k1 = slice((2 * kp + 1) * 128, (2 * kp + 2) * 128)
                MMS([(pb(b0), KT[:, h, k0], QT[:, h, qs], True, False),
                     (pb(b0), KR[:, k0], QR[:, h, qs], False, True),
                     (pb(b1), KT[:, h, k1], QT[:, h, qs], True, False),
                     (pb(b1), KR[:, k1], QR[:, h, qs], False, True)],
                    [KT_t[(2 * kp) // 4], QT_t[qg], KR_t[(2 * kp) // 4], QR_t[qg]], [PS.t[b0], PS.t[b1]])

            issue_S_pair(0)
            n = 0
            pending = []
            for pi, (bi, kp) in enumerate(pairs_):
                h, qg = blocks[bi]
                if pi + 1 < len(pairs_):
                    issue_S_pair(pi + 1)
                ob = obanks[bi % 2]
                qs = slice(qg * 512, (qg + 1) * 512)
                for u in range(2):
                    kt = 2 * kp + u
                    p_ap, p_t = PT[n % NPT]
                    sbk = sbanks[(2 * pi + u) % 4]
                    ACT(p_ap, pb(sbk), AF.Exp, [PS.t[sbk]], [p_t], scale=ATT_SCALE)
                    MMS([(pb(ob), VM[:, kt, h * 128:(h + 1) * 128], p_ap, kt == 0, kt == NT - 1)], [p_t, VM_t[kt]], [PS.t[ob]])
                    n += 1
                if kp % 4 == 3:
                    items = []
                    rd_ = []
                    for q8 in range(2):
                        for jj in range(4):
                            p_ap, p_t = PT[(n - 8 + 4 * q8 + jj) % NPT]
                            items.append((pb(d4bank)[32 * jj:32 * (jj + 1), :], ones[:, 256:288], p_ap,
                                          (kp == 3 and q8 == 0), (kp == NT // 2 - 1 and q8 == 1), (0, 32 * jj)))
                            rd_.append(p_t)
                    MMS(items, rd_ + [t_const], [PS.t[d4bank]])
                if kp == NT // 2 - 1:
                    COPY("act", D4[0], pb(d4bank), [PS.t[d4bank]], [D4[1]])
                    TT("dve", D4L[0], pb(d4bank), D4[0], ALU.subtract, [PS.t[d4bank], D4[1]], [D4L[1]])
                    pending.append((bi, h, qg, ob))
                if (kp == 1 or pi == len(pairs_) - 1) and pending:
                    ebi, eh, eqg, eob = pending.pop(0)
                    eqs = slice(eqg * 512, (eqg + 1) * 512)
                    rd, rd_t = RD[ebi % 2]
                    tq, tq_t = TQ4[ebi % 2]
                    MM(dbbank, pb(dbbank), [(selb[:], D4[0]), (selb[:], D4L[0])], [D4[1], D4L[1], t_sel])
                    S.op("dve", lambda e, rd=rd: e.reciprocal(out=rd, in_=pb(dbbank)), reads=[PS.t[dbbank]], writes=[rd_t])
                    TT("dve", tq, pb(eob), rd, ALU.mult, [PS.t[eob], rd_t], [tq_t])
                    TT("dve", GM[:, eh, eqs], tq, GM[:, eh, eqs], ALU.mult, [tq_t, GM_t[eqg]], [GM_t[eqg]])
            dump("mixm", GM, GM_t, [128, 4, SEQ], BF16)

        if stop_after >= 5:
            scr_off[0] = 0
            lng, lng_t = salloc("lng", 4096, F32, 5)
            lnb, lnb_t = salloc("lnb", 4096, F32, 5)
            junk, junk_t = salloc("junk", 2048, BF16, 5)
            DMA("sp", lng, lngb_d[0:1, :].partition_broadcast(128), None, writes=[lng_t])
            DMA("sp", lnb, lngb_d[1:2, :].partition_broadcast(128), None, writes=[lnb_t])
            NR = 4
            XR = [(XTA.ap(i * 4096, 4096, F32), XTA.tile(f"xr{i}", i * 4096, 4096, 5)) for i in range(NR)]
            ZZ = [(XTA.ap(16384 + i * 4096, 4096, F32), XTA.tile(f"z{i}", 16384 + i * 4096, 4096, 5)) for i in range(NR)]
            ZH = [(XTA.tile(f"zh{i}a", 16384 + i * 4096, 2048, 5), XTA.tile(f"zh{i}b", 16384 + i * 4096 + 2048, 2048, 5)) for i in range(NR)]
            sem_xr = [S.dmasem(f"xr{i}") for i in range(NR)]
            sem_ob = [S.dmasem(f"ob{i}") for i in range(NR)]
            st_t = [Tile(f"lnst{i}") for i in range(NR)]
            outd_t = [Tile(f"outd{i}") for i in range(NR)]

            def load_xr(tt):
                DMA("sp", XR[tt % NR][0], x_d[tt * 128:(tt + 1) * 128, :], sem_xr[tt % NR], writes=[XR[tt % NR][1]])

            def st5_A(tt):
                sl = tt % NR
                ts_ = slice(tt * 128, (tt + 1) * 128)
                g = tt // 4
                xr, xr_t = XR[sl]
                z, z_t = ZZ[sl]
                ky = [PS.next(), PS.next()]
                for half in range(2):
                    pairs = []
                    for kc in range(8):
                        lhs = gateg[:, kc, ts_] if kc < 4 else GM[:, kc - 4, ts_]
                        pairs.append((lhs, wout[:, kc, half * 512:(half + 1) * 512]))
                    MM(ky[half], pb(ky[half]), pairs, [gateg_t[g], GM_t[g], wout_t])
                    hs = slice(half * 512, (half + 1) * 512)
                    STT(z[:, hs], xr[:, hs], ALPHA, pb(ky[half]), ALU.mult, ALU.add, [xr_t, PS.t[ky[half]]], [z_t, ZH[sl][0], ZH[sl][1]])
                st = lnst[:, sl, :]
                S.op("act", lambda e, st=st, z=z: e.activation(out=z, in_=z, func=AF.Identity, accum_out=st[:, 0:1]),
                     reads=[z_t], writes=[z_t, st_t[sl]])
                S.op("act", lambda e, st=st, z=z: e.activation(out=junk, in_=z, func=AF.Square, accum_out=st[:, 1:2]),
                     reads=[z_t], writes=[junk_t, st_t[sl]])

            def st5_B(tt):
                sl = tt % NR
                z, z_t = ZZ[sl]
                st = lnst[:, sl, :]
                TS("dve", st[:, 12:13], st[:, 0:1], 1.0 / D_MODEL, None, ALU.mult, None, [st_t[sl]], [st_t[sl]])
                TT("dve", st[:, 2:3], st[:, 12:13], st[:, 12:13], ALU.mult, [st_t[sl]], [st_t[sl]])
                STT(st[:, 13:14], st[:, 1:2], 1.0 / D_MODEL, st[:, 2:3], ALU.mult, ALU.subtract, [st_t[sl]], [st_t[sl]])
                ACT(st[:, 14:15], st[:, 13:14], AF.Ln, [st_t[sl]], [st_t[sl]], bias=LN_EPS)
                ACT(st[:, 14:15], st[:, 14:15], AF.Exp, [st_t[sl]], [st_t[sl]], scale=-0.5)
                TS("dve", st[:, 15:16], st[:, 12:13], -1.0, st[:, 14:15], ALU.mult, ALU.mult, [st_t[sl]], [st_t[sl]])
                ACT(z, z, AF.Identity, [z_t, st_t[sl]], [z_t], scale=st[:, 14:15], bias=st[:, 15:16])

            def st5_C(tt):
                sl = tt % NR
                ts_ = slice(tt * 128, (tt + 1) * 128)
                z, z_t = ZZ[sl]
                za_t, zb_t = ZH[sl]
                TT("pool", z, z, lng, ALU.mult, [z_t, lng_t], [z_t, za_t, zb_t])
                TT("dve", z[:, 512:1024], z[:, 512:1024], lnb[:, 512:1024], ALU.add, [zb_t, lnb_t], [zb_t])
                TT("pool", z[:, 0:512], z[:, 0:512], lnb[:, 0:512], ALU.add, [za_t, lnb_t], [za_t])
                DMA("sp", out_d[ts_, :], z, sem_ob[sl], reads=[z_t, za_t, zb_t], writes=[outd_t[sl]])
                if tt + NR < NT:
                    load_xr(tt + NR)

            for tt in range(NR):
                load_xr(tt)
            for it in range(NT + 2):
                if it < NT:
                    st5_A(it)
                if 0 <= it - 1 < NT:
                    st5_B(it - 1)
                if 0 <= it - 2 < NT:
                    st5_C(it - 2)
            dump_tiles.extend(outd_t)

        S.op("sp", lambda e: e.nop(), reads=dump_tiles)
        S.emit()
    return nc, dump_d


_PROGRAM = None


def kernel(**inputs):
    global _PROGRAM
    if _PROGRAM is None:
        _PROGRAM = build_program()[0]
    nc = _PROGRAM
    shared = prep_shared(inputs)
    x = np.asarray(inputs["x"], np.float32)
    pos = np.asarray(inputs["positions"]).astype(np.int32)
    in_maps = []
    for b in range(BATCH):
        m = dict(shared)
        m["x"] = np.ascontiguousarray(x[b])
        m["pos"] = np.ascontiguousarray(pos[b:b + 1])
        in_maps.append(m)
    res = run_bass_kernel_spmd(nc, in_maps, core_ids=list(range(BATCH)))
    return np.stack([np.asarray(r["out"], np.float32) for r in res.results], axis=0)
```

```python
import math
from contextlib import ExitStack

import numpy as np
import concourse.bass as bass
import concourse.mybir as mybir
from concourse.bass_utils import run_bass_kernel_spmd

F32 = mybir.dt.float32
BF16 = mybir.dt.bfloat16
I32 = mybir.dt.int32
AF = mybir.ActivationFunctionType
ALU = mybir.AluOpType

D_MODEL = 1024
BATCH = 8
SEQ = 2048
NT = SEQ // 128
NG = SEQ // 512
RMS_EPS = 1e-6
LN_EPS = 1e-5
ALPHA = 2.0 ** 0.25
ATT_SCALE = 192.0 ** -0.5
IN_SPLITS = (256, 256, 512, 512, 16, 16, 256, 128, 64, 512)


class Tile:
    __slots__ = ("name", "w", "r", "inherit", "lo", "hi")

    def __init__(self, name, inherit=None, lo=0, hi=0):
        self.name = name
        self.w = {}
        self.r = {}
        self.inherit = list(inherit) if inherit else []
        self.lo, self.hi = lo, hi


class DmaSem:
    def __init__(self, sem, name):
        self.sem, self.name, self.count = sem, name, 0


class Op:
    __slots__ = ("engine", "fn", "idx", "signal", "semval", "waits", "dmasem", "dmacount", "is_dma")


class Sched:
    ENGS = ("pe", "act", "dve", "pool", "sp")

    def __init__(self, nc, es):
        self.nc = nc
        self.es = es
        self.ops = {e: [] for e in self.ENGS}
        self.sems = {e: es.enter_context(nc.semaphore("sem_" + e)) for e in ("pe", "act", "dve", "pool")}
        self.ndma = 0

    def dmasem(self, name):
        return DmaSem(self.es.enter_context(self.nc.semaphore("dq_" + name)), name)

    def _collect(self, reads, writes):
        deps = []
        for t in reads:
            deps.extend(t.w.values())
            deps.extend(t.inherit)
        for t in writes:
            deps.extend(t.w.values())
            deps.extend(t.r.values())
            deps.extend(t.inherit)
        return deps

    def op(self, engine, fn, reads=(), writes=()):
        o = Op()
        o.engine, o.fn, o.idx = engine, fn, len(self.ops[engine])
        o.signal, o.semval, o.is_dma, o.dmasem, o.dmacount = False, None, False, None, 0
        deps = self._collect(reads, writes)
        o.waits = self._filter(o, deps)
        self.ops[engine].append(o)
        me = ("eng", engine, o.idx)
        for t in reads:
            t.r[engine] = me
        for t in writes:
            t.w = {engine: me}
            t.r = {}
        return o

    def dma(self, queue, fn, dsem, reads=(), writes=()):
        o = Op()
        o.engine, o.fn, o.idx = queue, fn, len(self.ops[queue])
        o.signal, o.semval, o.is_dma = False, None, True
        dsem.count += 16
        o.dmasem, o.dmacount = dsem, dsem.count
        deps = self._collect(reads, writes)
        o.waits = self._filter(o, deps, dma=True)
        self.ops[queue].append(o)
        me = ("dma", dsem, dsem.count)
        key = "dma_" + dsem.name
        for t in reads:
            t.r[key] = me
        for t in writes:
            t.w = {key: me}
            t.r = {}
        return o

    def _filter(self, o, deps, dma=False):
        out = []
        seen = set()
        for d in deps:
            if d[0] == "eng":
                _, eng, idx = d
                if eng == o.engine and not dma:
                    if eng == "pe":
                        continue
                k = (eng, idx)
            else:
                k = (id(d[1]), d[2])
            if k in seen:
                continue
            seen.add(k)
            out.append(d)
            if d[0] == "eng":
                self.ops[d[1]][d[2]].signal = True
        return out

    def emit(self):
        nc = self.nc
        for e in ("pe", "act", "dve", "pool"):
            c = 0
            for o in self.ops[e]:
                if o.signal and not o.is_dma:
                    c += 1
                    o.semval = c
        sched = self

        def run(engname):
            def body(e):
                seen = {}
                for o in sched.ops[engname]:
                    best = {}
                    for d in o.waits:
                        if d[0] == "eng":
                            sem = sched.sems[d[1]]
                            val = sched.ops[d[1]][d[2]].semval
                            key = "e" + d[1]
                        else:
                            sem = d[1].sem
                            val = d[2]
                            key = "d" + d[1].name
                        if seen.get(key, 0) >= val:
                            continue
                        if key not in best or best[key][1] < val:
                            best[key] = (sem, val)
                    for key, (sem, val) in best.items():
                        e.wait_ge(sem, val)
                        seen[key] = val
                    ins = o.fn(e)
                    if o.is_dma:
                        ins.then_inc(o.dmasem.sem, 16)
                    elif o.signal:
                        ins.then_inc(sched.sems[engname], 1)
            return body

        with nc.Block() as block:
            block.tensor(run("pe"))
            block.scalar(run("act"))
            block.vector(run("dve"))
            block.gpsimd(run("pool"))
            block.sync(run("sp"))


class Arena:
    def __init__(self, tensor, ncols_bf16):
        self.t = tensor
        self.n = ncols_bf16
        self.tiles = []

    def tile(self, name, lo_b, nbytes, gen):
        hi_b = lo_b + nbytes
        assert hi_b <= self.n * 2, (name, lo_b, nbytes, self.n * 2)
        t = Tile(name, lo=lo_b, hi=hi_b)
        t.inherit = _LazyInherit(self, t, gen)
        self.tiles.append((t, gen))
        return t

    def ap(self, lo_b, nbytes, dtype):
        a = self.t[:, lo_b // 2:(lo_b + nbytes) // 2]
        if dtype != BF16:
            a = a.bitcast(dtype)
        return a


class _LazyInherit:
    def __init__(self, arena, tile, gen):
        self.arena, self.tile, self.gen = arena, tile, gen

    def __iter__(self):
        me = self.tile
        for t, g in self.arena.tiles:
            if g < self.gen and t.lo < me.hi and me.lo < t.hi:
                for d in t.w.values():
                    yield d
                for d in t.r.values():
                    yield d
                for d in t.inherit:
                    yield d


def _kblk(w):
    n = w.shape[1]
    return np.ascontiguousarray(w.reshape(8, 128, n).transpose(1, 0, 2))


def prep_shared(inp):
    w_in = np.asarray(inp["w_in"], np.float32)[0]
    cs = np.cumsum((0,) + IN_SPLITS)
    gq, gk, gv, gg, lrf, lrb, cq, ckv, kr, mg = [w_in[:, cs[i]:cs[i + 1]] for i in range(10)]
    krrot = np.concatenate([kr[:, 32:64], kr[:, 0:32]], axis=1)
    w_gla = _kblk(np.concatenate([lrf, lrb, gq, gk, gv, gg], axis=1))
    w_mla = _kblk(np.concatenate([cq, ckv, kr, kr, krrot, krrot, mg], axis=1))
    w_uq = np.asarray(inp["mla_w_uq"], np.float32)[0]
    nope = [w_uq[:, h * 192:h * 192 + 128] for h in range(4)]
    rope = [w_uq[:, h * 192 + 128:h * 192 + 192] for h in range(4)]
    rot = [np.concatenate([r[:, 32:64], r[:, 0:32]], axis=1) for r in rope]
    uq_all = np.concatenate(nope + rope + rot, axis=1)
    w_uq_arr = np.ascontiguousarray(uq_all.reshape(2, 128, 1024).transpose(1, 0, 2))
    w_ukv = np.asarray(inp["mla_w_ukv"], np.float32)[0]
    kcols = [w_ukv[:, h * 256:h * 256 + 128] for h in range(4)]
    vcols = [w_ukv[:, h * 256 + 128:h * 256 + 256] for h in range(4)]
    w_ukv_arr = np.ascontiguousarray(np.concatenate(kcols + vcols, axis=1))
    wg2pad = np.zeros((32, 512), np.float32)
    wg2pad[0:16, 0:256] = np.asarray(inp["gla_wg2_fwd"], np.float32)[0]
    wg2pad[16:32, 256:512] = np.asarray(inp["gla_wg2_bwd"], np.float32)[0]
    vecs = np.zeros((128, 16), np.float32)
    bgf = np.asarray(inp["gla_bg_fwd"], np.float32)[0]
    bgb = np.asarray(inp["gla_bg_bwd"], np.float32)[0]
    vecs[:, 0] = bgf[0:128]
    vecs[:, 1] = bgf[128:256]
    vecs[:, 2] = bgb[0:128]
    vecs[:, 3] = bgb[128:256]
    vecs[:, 4] = np.asarray(inp["gla_norm_g"], np.float32)[0]
    qn = np.asarray(inp["mla_q_norm_g"], np.float32)[0]
    vecs[:, 5] = qn[0:128]
    vecs[:, 6] = qn[128:256]
    vecs[:, 7] = np.asarray(inp["mla_kv_norm_g"], np.float32)[0]
    inv_freq = (10000.0 ** (-np.arange(32, dtype=np.float32) / np.float32(32))).astype(np.float32)
    vecs[:, 8] = np.tile(inv_freq, 4)
    w_out = _kblk(np.asarray(inp["w_out"], np.float32)[0])
    lngb = np.stack([np.asarray(inp["ln_g"], np.float32)[0], np.asarray(inp["ln_b"], np.float32)[0]])
    return dict(w_gla=w_gla, w_mla=w_mla, w_uq=w_uq_arr, w_ukv=w_ukv_arr, wg2pad=wg2pad,
                vecs=vecs, w_out=w_out, lngb=np.ascontiguousarray(lngb))


class PsumPool:
    def __init__(self, nc, es):
        self.all = es.enter_context(nc.psum_tensor("pall", [128, 4096], F32))
        self.t = [Tile(f"pb{i}") for i in range(8)]
        self.i = 0
        self.skip = set()

    def next(self):
        while True:
            k = self.i
            self.i = (self.i + 1) % 8
            if k not in self.skip:
                return k


TWO_PI = 2.0 * math.pi
NXS = 4


def build_program(stop_after=99, dumps=()):
    nc = bass.Bass("TRN2", target_bir_lowering=False)
    dram = lambda name, shape, dt, kind="ExternalInput": nc.dram_tensor(name, list(shape), dt, kind=kind).ap()
    x_d = dram("x", [SEQ, D_MODEL], F32)
    pos_d = dram("pos", [1, SEQ], I32)
    w_gla_d = dram("w_gla", [128, 8, 1568], F32)
    w_mla_d = dram("w_mla", [128, 8, 1152], F32)
    w_uq_d = dram("w_uq", [128, 2, 1024], F32)
    w_ukv_d = dram("w_ukv", [128, 1024], F32)
    wg2_d = dram("wg2pad", [32, 512], F32)
    vecs_d = dram("vecs", [128, 16], F32)
    w_out_d = dram("w_out", [128, 8, 1024], F32)
    lngb_d = dram("lngb", [2, D_MODEL], F32)
    out_d = dram("out", [SEQ, D_MODEL], F32, kind="ExternalOutput")
    dump_d = {}

    with ExitStack() as es:
        S = Sched(nc, es)
        sb = lambda name, shape, dt: es.enter_context(nc.sbuf_tensor("s_" + name, list(shape), dt))
        PS = PsumPool(nc, es)
        pb = lambda k: PS.all[:, k * 512:(k + 1) * 512]
        pbb = lambda k: PS.all[:, k * 512:(k + 1) * 512].bitcast(BF16)
        pb2 = lambda k: PS.all[:, k * 512:(k + 2) * 512]

        def MM(k, out, pairs, reads):
            pairs = list(pairs)

            def f(e):
                n = len(pairs)
                for i, (l, r) in enumerate(pairs):
                    ins = e.matmul(out, lhsT=l, rhs=r, start=(i == 0), stop=(i == n - 1))
                return ins
            S.op("pe", f, reads=reads, writes=[PS.t[k]])

        def MMS(items, reads, writes):
            items = list(items)

            def f(e):
                for it in items:
                    out, l, r, st, sp = it[:5]
                    kw = {}
                    if len(it) > 5 and it[5] is not None:
                        kw["tile_position"] = it[5]
                    if len(it) > 6 and it[6]:
                        kw["skip_group_check"] = True
                    ins = e.matmul(out, lhsT=l, rhs=r, start=st, stop=sp, **kw)
                return ins
            S.op("pe", f, reads=reads, writes=writes)

        def TR(k, items, reads):
            items = list(items)

            def f(e):
                for (out, in_) in items:
                    ins = e.transpose(out=out, in_=in_, identity=ident[:])
                return ins
            S.op("pe", f, reads=list(reads) + [t_const], writes=[PS.t[k]])

        def ACT(out, in_, func, reads, writes, scale=None, bias=None):
            kw = {}
            if scale is not None:
                kw["scale"] = scale
            if bias is not None:
                kw["bias"] = bias
            S.op("act", lambda e: e.activation(out=out, in_=in_, func=func, **kw), reads=reads, writes=writes)

        def COPY(eng, out, in_, reads, writes):
            if eng == "act":
                S.op("act", lambda e: e.copy(out=out, in_=in_), reads=reads, writes=writes)
            else:
                S.op(eng, lambda e: e.tensor_copy(out=out, in_=in_), reads=reads, writes=writes)

        def TT(eng, out, in0, in1, op, reads, writes):
            S.op(eng, lambda e: e.tensor_tensor(out=out, in0=in0, in1=in1, op=op), reads=reads, writes=writes)

        def TS(eng, out, in0, s1, s2, op0, op1, reads, writes):
            if op1 is None:
                S.op(eng, lambda e: e.tensor_scalar(out=out, in0=in0, scalar1=s1, scalar2=None, op0=op0), reads=reads, writes=writes)
            else:
                S.op(eng, lambda e: e.tensor_scalar(out=out, in0=in0, scalar1=s1, scalar2=s2, op0=op0, op1=op1), reads=reads, writes=writes)

        def STT(out, in0, scalar, in1, op0, op1, reads, writes):
            S.op("dve", lambda e: e.scalar_tensor_tensor(out=out, in0=in0, scalar=scalar, in1=in1, op0=op0, op1=op1),
                 reads=reads, writes=writes)

        dma_ctr = [0]

        def DMA(queue, out, in_, dsem, reads=(), writes=()):
            if dsem is None:
                dma_ctr[0] += 1
                dsem = S.dmasem(f"o{dma_ctr[0]}")
            S.dma(queue, lambda e: e.dma_start(out=out, in_=in_), dsem, reads=reads, writes=writes)

        ident = sb("ident", [128, 128], BF16)
        identf = sb("identf", [128, 128], F32)
        ones = sb("ones", [128, 384], BF16)
        sel = sb("sel", [128, 128], F32)
        selb = sb("selb", [128, 128], BF16)
        mask01 = sb("mask01", [128, 512], F32)
        maskFB = sb("maskFB", [128, 256], F32)
        vecs = sb("vecs", [128, 16], F32)
        negbg = sb("negbg", [128, 4], F32)
        decF = sb("decF", [128, 16, 2], F32)
        decB = sb("decB", [128, 16, 2], F32)
        decS = sb("decS", [128, 16, 4], F32)
        tmp4 = sb("tmp4", [128, 8], F32)
        lnst = sb("lnst", [128, 4, 16], F32)
        wg2 = sb("wg2", [32, 512], BF16)
        wuq = sb("wuq", [128, 2, 1024], BF16)
        wukv = sb("wukv", [128, 1024], BF16)
        cs2 = sb("cs2", [128, 2, SEQ], BF16)
        xs = [sb(f"xs{i}", [128, D_MODEL], BF16) for i in range(NXS)]
        gateg = sb("gateg", [128, 4, SEQ], BF16)
        XTA = Arena(sb("xta", [128, 16384], BF16), 16384)
        WAR = Arena(sb("war", [128, 8 * 1568], BF16), 8 * 1568)
        ARE = Arena(sb("arena", [128, 43008], BF16), 43008)
        SCR = Arena(sb("scr", [128, 11360], BF16), 11360)

        t_const = Tile("const")
        t_vecs = Tile("vecs")
        t_wg2 = Tile("wg2")
        t_wuq = Tile("wuq")
        t_wukv = Tile("wukv")
        t_cs2 = []
        t_dec = Tile("dec")
        xs_t = [Tile(f"xs{i}") for i in range(NXS)]
        gateg_t = [Tile(f"gateg{g}") for g in range(NG)]

        xT = XTA.ap(0, 32768, BF16).rearrange("p (k t) -> p k t", k=8)
        xT_t = [XTA.tile(f"xT{tt}", 0, 32768, 0) for tt in range(NT)]
        wgla = WAR.ap(0, 8 * 1568 * 2, BF16).rearrange("p (k c) -> p k c", k=8)
        wgla_t = {n: WAR.tile("wgla_" + n, 0, 8 * 1568 * 2, 1) for n in ("lr", "qk", "v", "g")}

        sem_xs = [S.dmasem(f"xs{i}") for i in range(NXS)]
        sem_dump = S.dmasem("dump")
        dump_tiles = []

        def dump(name, ap, tiles, shape, dt):
            if name not in dumps:
                return
            d = nc.dram_tensor("dump_" + name, list(shape), dt, kind="ExternalOutput").ap()
            dump_d[name] = d
            t = Tile("dump_" + name)
            dump_tiles.append(t)
            DMA("sp", d, ap, sem_dump, reads=list(tiles), writes=[t])

        def load_x(tt):
            sl = tt % NXS
            DMA("pool", xs[sl][:], x_d[tt * 128:(tt + 1) * 128, :], sem_xs[sl], writes=[xs_t[sl]])
        NF = 4
        xf = [ARE.ap(68608 + i * 4096, 4096, F32) for i in range(NF)]
        xf_t = [ARE.tile(f"xf{i}", 68608 + i * 4096, 4096, 0) for i in range(NF)]
        DMA("sp", vecs[:], vecs_d, None, writes=[t_vecs])
        wpieces = [("lr", 0, 32), ("v", 544, 800), ("v", 800, 1056), ("qk", 32, 288), ("qk", 288, 544)]
        wst = []

        def wstage(pi_):
            n_, a_, b_ = wpieces[pi_]
            lo_ = 0 if pi_ == 0 else 1024 + (pi_ - 1) * 8192
            nb_ = 1024 if pi_ == 0 else 8192
            st_ap = ARE.ap(lo_, nb_, F32).rearrange("p (k c) -> p k c", k=8)
            st_t = ARE.tile(f"wst{pi_}", lo_, nb_, 0)
            wst.append((st_ap, st_t))
            DMA("sp", st_ap, w_gla_d[:, :, a_:b_], None, writes=[st_t])
        wstage(0)
        for i in range(NF):
            DMA("sp", xf[i], x_d[i * 128:(i + 1) * 128, :], None, writes=[xf_t[i]])
        DMA("pool", wg2[:], wg2_d, None, writes=[t_wg2])
        for pi_ in range(1, len(wpieces)):
            wstage(pi_)

        def wcast(pi_):
            n_, a_, b_ = wpieces[pi_]
            COPY("act", wgla[:, :, a_:b_], wst[pi_][0], [wst[pi_][1]], [wgla_t[n_]])

        t_identf = Tile("identf")
        t_m01 = Tile("m01")
        t_mfb = Tile("mfb")
        S.op("pool", lambda e: e.memset(identf[:], 0.0), writes=[t_identf])
        S.op("pool", lambda e: e.affine_select(out=identf[:], in_=identf[:], pattern=[[-1, 128]], compare_op=ALU.not_equal,
                                               fill=1.0, base=0, channel_multiplier=1), reads=[t_identf], writes=[t_identf])
        S.op("pool", lambda e: e.tensor_copy(out=ident[:], in_=identf[:]), reads=[t_identf], writes=[t_const])
        S.op("pool", lambda e: e.memset(ones[:, 0:128], 1.0 / 256.0), writes=[t_const])
        S.op("pool", lambda e: e.memset(ones[:, 128:256], 1.0 / 128.0), writes=[t_const])
        S.op("pool", lambda e: e.memset(ones[:, 256:384], 1.0), writes=[t_const])
        t_sel = Tile("sel")
        S.op("pool", lambda e: e.memset(sel[:], 0.0), writes=[t_sel])
        for p0 in (0, 32, 64, 96):
            S.op("pool", lambda e, p0=p0: e.affine_select(out=sel[:], in_=sel[:], pattern=[[0, 128]], compare_op=ALU.not_equal,
                                                          fill=1.0, base=-p0, channel_multiplier=1), reads=[t_sel], writes=[t_sel])
        S.op("pool", lambda e: e.tensor_copy(out=selb[:], in_=sel[:]), reads=[t_sel], writes=[t_sel])
        S.op("pool", lambda e: e.memset(mask01[:], 1.0), writes=[t_m01])
        for c in range(4):
            S.op("pool", lambda e, c=c: e.memset(mask01[:, c * 128:c * 128 + 1], 0.0), reads=[t_m01], writes=[t_m01])
        S.op("pool", lambda e: e.memset(maskFB[:], 1.0), writes=[t_mfb])
        S.op("pool", lambda e: e.affine_select(out=maskFB[:, 0:128], in_=maskFB[:, 0:128], pattern=[[1, 128]],
                                               compare_op=ALU.is_ge, fill=0.0, base=0, channel_multiplier=-1), reads=[t_mfb], writes=[t_mfb])
        S.op("pool", lambda e: e.affine_select(out=maskFB[:, 128:256], in_=maskFB[:, 128:256], pattern=[[-1, 128]],
                                               compare_op=ALU.is_gt, fill=0.0, base=0, channel_multiplier=1), reads=[t_mfb], writes=[t_mfb])
        S.op("pool", lambda e: e.memset(tmp4[:, 0:1], 0.0), reads=[t_m01, t_mfb], writes=[t_const])

        TS("dve", negbg[:], vecs[:, 0:4], -1.0, None, ALU.mult, None, [t_vecs], [t_const])

        def transpose_x(tt):
            if tt < NF:
                k = 2 * tt
                items = [(pb2(k)[:, kc * 128:(kc + 1) * 128], xf[tt][:, kc * 128:(kc + 1) * 128]) for kc in range(8)]

                def f(e):
                    for (o, i_) in items:
                        ins = e.transpose(out=o, in_=i_, identity=identf[:])
                    return ins
                S.op("pe", f, reads=[xf_t[tt], t_identf], writes=[PS.t[k], PS.t[k + 1]])
                for half in range(2):
                    COPY("act" if half == 0 else "dve", xT[:, 4 * half:4 * half + 4, tt * 128:(tt + 1) * 128],
                         pb(k + half).rearrange("p (k t) -> p k t", k=4), [PS.t[k + half]], [xT_t[tt]])
                return
            sl = tt % NXS
            k = PS.next()
            TR(k, [(pbb(k)[:, kc * 128:(kc + 1) * 128], xs[sl][:, kc * 128:(kc + 1) * 128]) for kc in range(8)], [xs_t[sl]])
            COPY("act" if (tt % 2 == 0 or tt < 4) else "dve", xT[:, :, tt * 128:(tt + 1) * 128],
                 pbb(k).rearrange("p (k t) -> p k t", k=8), [PS.t[k]], [xT_t[tt]])

        wsl = {"lr": (0, 32), "qk": (32, 544), "v": (544, 1056), "g": (1056, 1568)}

        def load_wgla(n):
            a, b = wsl[n]
            DMA("pool", wgla[:, :, a:b], w_gla_d[:, :, a:b], None, writes=[wgla_t[n]])

        dump("xT", xT, xT_t, [128, 8, SEQ], BF16)
        dump("cs2", cs2[:], t_cs2, [128, 2, SEQ], BF16)


        QD = {}
        QD_t = {}
        for i, n in enumerate(("qdf", "qdb", "kdf", "kdb")):
            QD[n] = ARE.ap(i * 8192, 8192, BF16).rearrange("p (j t) -> p j t", j=2)
            QD_t[n] = [ARE.tile(f"{n}{g}", i * 8192, 8192, 1) for g in range(NG)]
        VG = ARE.ap(32768, 16384, BF16).rearrange("p (c v) -> p c v", c=NT)
        VG_t = [ARE.tile(f"vg{tt}", 32768, 16384, 1) for tt in range(NT)]
        SST = ARE.ap(49152, 16384, BF16).rearrange("p (d c v) -> p d c v", d=2, c=NT)
        SST_t = [[ARE.tile(f"sst{d}_{c}", 49152, 16384, 1) for c in range(NT)] for d in range(2)]
        KDT = [(ARE.ap(65536 + i * 1024, 1024, BF16), ARE.tile(f"kdt{i}", 65536 + i * 1024, 1024, 1)) for i in range(3)]

        scr_off = [0]

        def salloc(name, nbytes, dt, gen):
            lo = scr_off[0]
            scr_off[0] += nbytes
            assert scr_off[0] <= 22720, (name, scr_off[0])
            return SCR.ap(lo, nbytes, dt), SCR.tile(name, lo, nbytes, gen)

        lrT = [salloc(f"lrT{i}", 1024, BF16, 1) for i in range(2)]
        spf = [salloc(f"spf{i}", 2048, F32, 1) for i in range(2)]
        spb = [salloc(f"spb{i}", 2112, F32, 1) for i in range(2)]
        CF = spf
        CB = [salloc(f"CB{i}", 2048, F32, 1) for i in range(2)]
        ET = [salloc(f"E{i}", 2048, F32, 1) for i in range(4)]
        for sp_ap, sp_t in spb:
            S.op("pool", lambda e, sp_ap=sp_ap: e.memset(sp_ap, 0.0), writes=[sp_t])

        LN8 = math.log(0.125)
        ring = [0]

        def phase1_group(g):
            gs = slice(g * 512, (g + 1) * 512)
            xg_t = [xT_t[4 * g + i] for i in range(4)]
            k = PS.next()
            MM(k, pb(k)[0:32, :], [(wgla[:, kc, 0:32], xT[:, kc, gs]) for kc in range(8)], xg_t + [wgla_t["lr"]])
            lr_ap, lr_t = lrT[g % 2]
            COPY("dve" if g == 0 else "act", lr_ap[0:32, :], pb(k)[0:32, :], [PS.t[k]], [lr_t])
            for i in range(4):
                tt = 4 * g + i
                k = PS.next()
                MM(k, pb(k), [(xT[:, kc, tt * 128:(tt + 1) * 128], wgla[:, kc, 544:1056]) for kc in range(8)], [xT_t[tt], wgla_t["v"]])
                COPY("dve" if (i % 2 == 0 or g == 0) else "act", VG[:, tt, :], pb(k), [PS.t[k]], [VG_t[tt]])
            for j in range(2):
                r = ring[0] % 2
                ring[0] += 1
                spf_ap, spf_t = spf[r]
                spb_ap, spb_t = spb[r]
                CF_ap, CF_t = CF[r]
                CB_ap, CB_t = CB[r]
                for d in range(2):
                    k = PS.next()
                    MM(k, pb(k), [(wg2[0:32, d * 256 + j * 128:d * 256 + (j + 1) * 128], lr_ap[0:32, :])], [lr_t, t_wg2])
                    dst = spf_ap if d == 0 else spb_ap[:, 1:513]
                    dt_ = spf_t if d == 0 else spb_t
                    ACT(dst, pb(k), AF.Exp, [PS.t[k], t_const], [dt_], scale=-1.0, bias=negbg[:, 2 * d + j:2 * d + j + 1])
                    ACT(dst, dst, AF.Ln, [dt_], [dt_], bias=1.0)
                S.op("dve", lambda e, CF_ap=CF_ap, spf_ap=spf_ap: e.tensor_tensor_scan(
                    out=CF_ap, data0=mask01[:], data1=spf_ap, initial=0.0, op0=ALU.mult, op1=ALU.add),
                    reads=[spf_t, t_const], writes=[CF_t])
                S.op("dve", lambda e, CB_ap=CB_ap, spb_ap=spb_ap: e.tensor_tensor_scan(
                    out=CB_ap, data0=spb_ap[:, 0:512], data1=mask01[:], initial=0.0, op0=ALU.add, op1=ALU.mult),
                    reads=[spb_t, t_const], writes=[CB_t])
                ACT(decS[:, 4 * g:4 * g + 4, j], CF_ap[:, 127:512:128], AF.Exp, [CF_t], [t_dec], scale=-1.0 / 16.0)
                TT("dve", tmp4[:, 4 * j:4 * j + 4], CB_ap[:, 127:512:128], spb_ap[:, 128:513:128], ALU.add, [CB_t, spb_t], [t_dec])
                for ci in range(4):
                    c = 4 * g + ci
                    if c <= NT - 2:
                        ACT(decS[:, NT - 2 - c, 2 + j:3 + j], tmp4[:, 4 * j + ci:4 * j + ci + 1], AF.Exp, [t_dec], [t_dec], scale=-1.0 / 16.0)
                especs = [(CF_ap, CF_t, -1.0 / 16.0, LN8), (CF_ap, CF_t, 1.0 / 16.0, None),
                          (CB_ap, CB_t, 1.0 / 16.0, LN8), (CB_ap, CB_t, -1.0 / 16.0, None)]
                for i, (src, st, sc, bi) in enumerate(especs):
                    ACT(ET[i][0], src, AF.Exp, [st], [ET[i][1]], scale=sc, bias=bi)
                for qk in range(2):
                    k = PS.next()
                    c0 = 32 + qk * 256 + j * 128
                    MM(k, pb(k), [(wgla[:, kc, c0:c0 + 128], xT[:, kc, gs]) for kc in range(8)], xg_t + [wgla_t["qk"]])
                    for d in range(2):
                        n = ("qd", "kd")[qk] + ("f", "b")[d]
                        ei = d * 2 + qk
                        TT("dve", QD[n][:, j, gs], pb(k), ET[ei][0], ALU.mult, [PS.t[k], ET[ei][1]], [QD_t[n][g]])


            for h in range(4):
                k = PS.next()
                MM(k, pb(k), [(wgla[:, kc, 1056 + h * 128:1056 + (h + 1) * 128], xT[:, kc, gs]) for kc in range(8)], xg_t + [wgla_t["g"]])
                ACT(gateg[:, h, gs], pb(k), AF.Silu, [PS.t[k]], [gateg_t[g]])


        def tr_group(g):
            for i in range(4):
                tt = 4 * g + i
                transpose_x(tt)
                if tt >= NF and tt + NXS < NT:
                    load_x(tt + NXS)

        tr_group(0)
        for tt in range(NF, NF + NXS):
            sl_ = tt % NXS
            DMA("pool", xs[sl_][:], x_d[tt * 128:(tt + 1) * 128, :], sem_xs[sl_], reads=[xT_t[3]], writes=[xs_t[sl_]])
        DMA("pool", wgla[:, :, 1056:1568], w_gla_d[:, :, 1056:1568], None, reads=[xT_t[2]], writes=[wgla_t["g"]])
        for pi_ in range(len(wpieces)):
            wcast(pi_)
        phase1_group(0)
        tr_group(1)
        rA = ARE.ap(50176, 2048, F32)
        rB = ARE.ap(52224, 2048, F32)
        rC = ARE.ap(54272, 2048, I32)
        rD = ARE.ap(56320, 2048, F32)
        rP = ARE.ap(58368, 2048, I32)
        rT = ARE.ap(60416, 3072, BF16).rearrange("p (w t) -> p w t", w=3)
        t_r = {n: ARE.tile("rope_" + n, lo, 3072 if n == "T" else 2048, 0) for n, lo in (("A", 50176), ("B", 52224), ("C", 54272), ("D", 56320), ("P", 58368), ("T", 60416))}
        SIN_S = TWO_PI * (1.0 - 2e-7)
        SIN_B = -math.pi * (1.0 - 2e-7)
        sem_rp = S.dmasem("ropepos")
        for blk in range(4):
            S.dma("sp", lambda e, blk=blk: e.dma_start(out=rP[blk * 32:(blk + 1) * 32, :],
                                                       in_=pos_d[0:1, blk * 512:(blk + 1) * 512].partition_broadcast(32)),
                  sem_rp, writes=[t_r["P"]])
        COPY("dve", rA, rP, [t_r["P"]], [t_r["A"]])
        TS("dve", rA, rA, vecs[:, 8:9], None, ALU.mult, None, [t_r["A"], t_vecs], [t_r["A"]])
        for which, off in ((0, 0.75), (1, 0.5)):
            TS("dve", rB, rA, 1.0 / TWO_PI, off, ALU.mult, ALU.add, [t_r["A"]], [t_r["B"]])
            COPY("dve", rC, rB, [t_r["B"]], [t_r["C"]])
            COPY("dve", rD, rC, [t_r["C"]], [t_r["D"]])
            TT("dve", rB, rB, rD, ALU.subtract, [t_r["B"], t_r["D"]], [t_r["B"]])
            STT(rD, rB, 0.0, rB, ALU.is_lt, ALU.add, [t_r["B"]], [t_r["D"]])
            ACT(rT[:, which, :], rD, AF.Sin, [t_r["D"]], [t_r["T"]], scale=SIN_S, bias=SIN_B)
            if which == 1:
                ACT(rT[:, 2, :], rD, AF.Sin, [t_r["D"]], [t_r["T"]], scale=-SIN_S, bias=-SIN_B)
        for blk in range(4):
            for rep in range(4):
                tcs = Tile(f"cs2_{blk}_{rep}")
                t_cs2.append(tcs)
                srcv = rT[blk * 32:(blk + 1) * 32, 0:3:2, :] if rep % 2 == 0 else rT[blk * 32:(blk + 1) * 32, 0:2, :]
                DMA("sp", cs2[rep * 32:(rep + 1) * 32, :, blk * 512:(blk + 1) * 512], srcv, None,
                    reads=[t_r["T"]], writes=[tcs])

        for g in range(1, NG):
            if g + 1 < NG:
                tr_group(g + 1)
            phase1_group(g)
        DMA("pool", wuq[:], w_uq_d, None, writes=[t_wuq])
        DMA("pool", wukv[:], w_ukv_d, None, writes=[t_wukv])

        for n in ("qdf", "qdb", "kdf", "kdb"):
            dump(n, QD[n], QD_t[n], [128, 2, SEQ], BF16)
        dump("vg", VG, VG_t, [128, NT, 512], BF16)
        dump("gateg", gateg[:], gateg_t, [128, 4, SEQ], BF16)

        if stop_after >= 3:
            wmla = WAR.ap(0, 8 * 1152 * 2, BF16).rearrange("p (k c) -> p k c", k=8)
            wmla_t = {n: WAR.tile("wmla_" + n, 0, 8 * 1152 * 2, 3) for n in ("a", "g")}
            DMA("pool", wmla[:, :, 0:640], w_mla_d[:, :, 0:640], None, writes=[wmla_t["a"]])
            DMA("pool", wmla[:, :, 640:1152], w_mla_d[:, :, 640:1152], None, writes=[wmla_t["g"]])

        if stop_after >= 2:
            scr_off[0] = 0
            Xst, Xst_t = salloc("Xst", 2048, F32, 2)
            X2, X2_t = salloc("X2", 2048, F32, 2)
            ABT = [salloc(f"ABT{i}", 2048, BF16, 2) for i in range(2)]
            SQ = [salloc(f"sq{i}", 1024, BF16, 2) for i in range(2)]
            RS = [salloc(f"rs{i}", 2048, F32, 2) for i in range(3)]
            TQ = [salloc(f"tq{i}", 2048, F32, 2) for i in range(3)]
            S.op("dve", lambda e: e.memset(Xst, 0.0), writes=[Xst_t])
            S.op("pool", lambda e: e.memset(SST[:, 0, 0, :], 0.0), writes=[SST_t[0][0]])
            S.op("pool", lambda e: e.memset(SST[:, 1, NT - 1, :], 0.0), writes=[SST_t[1][NT - 1]])
            NS = NT - 1
            tr_k = {}
            t_decS = t_dec

            def chain_tr(s):
                cf, cb = s, NT - 1 - s
                kd_ap, kd_t = KDT[s % 3]
                k = PS.next()
                tr_k[s] = k
                items = []
                for d, c, n in ((0, cf, "kdf"), (1, cb, "kdb")):
                    for j in range(2):
                        q = d * 2 + j
                        items.append((pbb(k)[:, q * 128:(q + 1) * 128], QD[n][:, j, c * 128:(c + 1) * 128]))
                TR(k, items, [QD_t["kdf"][cf // 4], QD_t["kdb"][cb // 4]])
                COPY("act", kd_ap, pbb(k)[:, 0:512], [PS.t[k]], [kd_t])

            def chain_u(s):
                cf, cb = s, NT - 1 - s
                kd_ap, kd_t = KDT[s % 3]
                k2 = PS.next()
                items = []
                for d, c in ((0, cf), (1, cb)):
                    for j in range(2):
                        q = d * 2 + j
                        for i in range(2):
                            items.append((pb(k2)[i * 64:(i + 1) * 64, q * 128:(q + 1) * 128],
                                          kd_ap[:, q * 128 + i * 64:q * 128 + (i + 1) * 64],
                                          VG[:, c, (2 * j + i) * 128:(2 * j + i + 1) * 128], True, True))
                MMS(items, [kd_t, VG_t[cf], VG_t[cb]], [PS.t[k2]])
                TT("dve", X2, pb(k2), Xst, ALU.add, [PS.t[k2], Xst_t], [X2_t])
                TT("dve", Xst.rearrange("p (q v) -> p q v", q=4), X2.rearrange("p (q v) -> p q v", q=4),
                   decS[:, s, :].unsqueeze(2).broadcast_to([128, 4, 128]), ALU.mult, [X2_t, t_decS], [Xst_t])
                COPY("act", SST[:, 0, s + 1, :], Xst[:, 0:256], [Xst_t], [SST_t[0][s + 1]])
                COPY("act", SST[:, 1, NT - 2 - s, :], Xst[:, 256:512], [Xst_t], [SST_t[1][NT - 2 - s]])

            chain_tr(0)
            chain_tr(1)
            for s in range(NS):
                if s + 2 < NS:
                    chain_tr(s + 2)
                chain_u(s)
            dump("sst", SST, SST_t[0] + SST_t[1], [128, 2, NT, 256], BF16)

        if stop_after >= 2.5:
            hv = lambda ap: ap.rearrange("p (h t) -> p h t", h=4)
            abank = [0, 1]
            obank = [(2, 3), (4, 5)]
            mbanks = [6, 7]
            PS.i = 0
            mask4 = maskFB[:].rearrange("p (d t) -> p d t", d=2).unsqueeze(2).broadcast_to([128, 2, 2, 128])

            def stage_A(c):
                cs_ = slice(c * 128, (c + 1) * 128)
                g = c // 4
                ka, kb_ = abank
                items = []
                for di, (qn, kn) in enumerate((("qdf", "kdf"), ("qdb", "kdb"))):
                    for j in range(2):
                        for i, kk in ((0, ka), (1, kb_)):
                            col = (di * 2 + j) * 128
                            items.append((pb(kk)[:, col:col + 128], QD[kn][i * 64:(i + 1) * 64, j, cs_],
                                          QD[qn][i * 64:(i + 1) * 64, j, cs_], True, True))
                MMS(items, [QD_t[n][g] for n in ("qdf", "kdf", "qdb", "kdb")], [PS.t[ka], PS.t[kb_]])
                A_ = ABT[c % 2]
                for i, kk in ((0, ka), (1, kb_)):
                    outv = A_[0].rearrange("p (d j i t) -> p d j i t", d=2, j=2, i=2)[:, :, :, i, :]
                    TT("dve", outv, pb(kk).rearrange("p (d j t) -> p d j t", d=2, j=2), mask4, ALU.mult,
                       [PS.t[kk], t_const], [A_[1]])

            def stage_B(c):
                cs_ = slice(c * 128, (c + 1) * 128)
                g = c // 4
                kOe, kOo = obank[c % 2]
                abt = ABT[c % 2]
                abv = abt[0].rearrange("p (d h t) -> p d h t", d=2, h=4)
                items = []
                for i, kO in ((0, kOe), (1, kOo)):
                    for j in range(2):
                        h = 2 * j + i
                        o = pb(kO)[:, j * 128:(j + 1) * 128]
                        v = VG[:, c, h * 128:(h + 1) * 128]
                        items.append((o, v, abv[:, 0, h, :], j == 0, False, None, True))
                        items.append((o, v, abv[:, 1, h, :], False, False, None, True))
                for j in range(2):
                    for d, qn in ((0, "qdf"), (1, "qdb")):
                        for i, kO in ((0, kOe), (1, kOo)):
                            o = pb(kO)[:, j * 128:(j + 1) * 128]
                            items.append((o, SST[i * 64:(i + 1) * 64, d, c, j * 128:(j + 1) * 128],
                                          QD[qn][i * 64:(i + 1) * 64, j, cs_], False, (j == 1 and d == 1), None, True))
                MMS(items, [VG_t[c], abt[1], SST_t[0][c], SST_t[1][c], QD_t["qdf"][g], QD_t["qdb"][g]], [PS.t[kOe], PS.t[kOo]])
                sq = SQ[c % 2]
                tq = TQ[c % 3]
                o2 = pb2(kOe).rearrange("p (b c) -> p b c", b=2)[:, :, 0:256]
                ACT(sq[0].rearrange("p (b c) -> p b c", b=2), o2, AF.Square, [PS.t[kOe], PS.t[kOo]], [sq[1]])
                COPY("act", tq[0].rearrange("p (b c) -> p b c", b=2), o2, [PS.t[kOe], PS.t[kOo]], [tq[1]])

            def stage_B2(c):
                sq = SQ[c % 2]
                rs = RS[c % 3]
                mbank = mbanks[c % 2]
                MM(mbank, pb(mbank), [(ones[:, 128:256], sq[0])], [sq[1], t_const])
                ACT(rs[0], pb(mbank), AF.Ln, [PS.t[mbank]], [rs[1]], bias=RMS_EPS)
                ACT(rs[0], rs[0], AF.Exp, [rs[1]], [rs[1]], scale=-0.5)

            def stage_C(c):
                cs_ = slice(c * 128, (c + 1) * 128)
                g = c // 4
                rs = RS[c % 3]
                tq = TQ[c % 3]
                STT(tq[0], tq[0], vecs[:, 4:5], rs[0], ALU.mult, ALU.mult, [rs[1], t_vecs, tq[1]], [tq[1]])
                gv = gateg[:, :, cs_].rearrange("p (j i) t -> p i j t", i=2)
                TT("pool", gv, tq[0].rearrange("p (i j t) -> p i j t", i=2, j=2), gv, ALU.mult, [tq[1], gateg_t[g]], [gateg_t[g]])

            for it in range(NT + 3):
                if it < NT:
                    stage_A(it)
                if 0 <= it - 1 < NT:
                    stage_B(it - 1)
                if 0 <= it - 2 < NT:
                    stage_B2(it - 2)
                if 0 <= it - 3 < NT:
                    stage_C(it - 3)
            dump("mixg", gateg[:], gateg_t, [128, 4, SEQ], BF16)

        if stop_after >= 3:
            KT = ARE.ap(0, 16384, BF16).rearrange("p (h t) -> p h t", h=4)
            KT_t = [ARE.tile(f"KT{g}", 0, 16384, 3) for g in range(NG)]
            QT = ARE.ap(16384, 16384, BF16).rearrange("p (h t) -> p h t", h=4)
            QT_t = [ARE.tile(f"QT{g}", 16384, 16384, 3) for g in range(NG)]
            VM = ARE.ap(32768, 16384, BF16).rearrange("p (c v) -> p c v", c=NT)
            VM_t = [ARE.tile(f"VM{tt}", 32768, 16384, 3) for tt in range(NT)]
            GM = ARE.ap(49152, 16384, BF16).rearrange("p (h t) -> p h t", h=4)
            GM_t = [ARE.tile(f"GM{g}", 49152, 16384, 3) for g in range(NG)]
            KR = ARE.ap(65536, 4096, BF16)
            KR_t = [ARE.tile(f"KR{g}", 65536, 4096, 3) for g in range(NG)]
            QR = ARE.ap(69632, 16384, BF16).rearrange("p (h t) -> p h t", h=4)
            QR_t = [ARE.tile(f"QR{g}", 69632, 16384, 3) for g in range(NG)]
            S.op("pool", lambda e: e.memset(ARE.ap(69632, 16384, BF16), 0.0), writes=QR_t)

            scr_off[0] = 0
            SQ3 = [[salloc(f"sq3_{i}_{r}", 1024, BF16, 3) for i in range(3)] for r in range(2)]
            RQ = [salloc("rq", 2048, F32, 3)] * 2
            RKV = [salloc("rkv", 2048, F32, 3)] * 2
            CQN = [salloc(f"cqn{r}", 2048, BF16, 3) for r in range(2)]
            CKVN = [salloc(f"ckvn{r}", 1024, BF16, 3) for r in range(2)]
            TRR = [salloc(f"trr{r}", 2048, F32, 3) for r in range(3)]
            rr = 0
            hold = [0, 1, 2]
            PS.skip = set(hold)

            def rope_combine(kA, kB, outs, out_t, gs):
                nonlocal rr
                t1, t1t = TRR[rr % 3]
                t2, t2t = TRR[(rr + 1) % 3]
                rr += 2
                TT("dve", t1, pb(kA), cs2[:, 0, gs], ALU.mult, [PS.t[kA]] + t_cs2, [t1t])
                TT("dve", t2, pb(kB), cs2[:, 1, gs], ALU.mult, [PS.t[kB]] + t_cs2, [t2t])
                for (p0, p1, out) in outs:
                    TT("pool", out, t1[p0:p1, :], t2[p0:p1, :], ALU.add, [t1t, t2t], [out_t])

            p3 = {}

            def A_units(g):
                gs = slice(g * 512, (g + 1) * 512)
                xg_t = [xT_t[4 * g + i] for i in range(4)]
                r = g % 2
                kk = [hold[bi] for bi in range(3)]
                p3[g] = kk
                U = []

                def u_c(bi):
                    k = kk[bi]
                    MM(k, pb(k), [(wmla[:, kc, bi * 128:(bi + 1) * 128], xT[:, kc, gs]) for kc in range(8)], xg_t + [wmla_t["a"]])
                    ACT(SQ3[r][bi][0], pb(k), AF.Square, [PS.t[k]], [SQ3[r][bi][1]])

                def u_gate(h):
                    k = PS.next()
                    MM(k, pb(k), [(wmla[:, kc, 640 + h * 128:640 + (h + 1) * 128], xT[:, kc, gs]) for kc in range(8)], xg_t + [wmla_t["g"]])
                    ACT(GM[:, h, gs], pb(k), AF.Silu, [PS.t[k]], [GM_t[g]])

                def u_stats():
                    kq = PS.next()
                    MM(kq, pb(kq), [(ones[:, 0:128], SQ3[r][0][0]), (ones[:, 0:128], SQ3[r][1][0])], [SQ3[r][0][1], SQ3[r][1][1], t_const])
                    ACT(RQ[r][0], pb(kq), AF.Ln, [PS.t[kq]], [RQ[r][1]], bias=RMS_EPS)
                    ACT(RQ[r][0], RQ[r][0], AF.Exp, [RQ[r][1]], [RQ[r][1]], scale=-0.5)
                    kv_ = PS.next()
                    MM(kv_, pb(kv_), [(ones[:, 128:256], SQ3[r][2][0])], [SQ3[r][2][1], t_const])
                    ACT(RKV[r][0], pb(kv_), AF.Ln, [PS.t[kv_]], [RKV[r][1]], bias=RMS_EPS)
                    ACT(RKV[r][0], RKV[r][0], AF.Exp, [RKV[r][1]], [RKV[r][1]], scale=-0.5)

                def u_kr():
                    kR = PS.next()
                    MM(kR, pb(kR), [(wmla[:, kc, 384:512], xT[:, kc, gs]) for kc in range(8)], xg_t + [wmla_t["a"]])
                    kS = PS.next()
                    MM(kS, pb(kS), [(wmla[:, kc, 512:640], xT[:, kc, gs]) for kc in range(8)], xg_t + [wmla_t["a"]])
                    rope_combine(kR, kS, [(0, 128, KR[:, gs])], KR_t[g], gs)

                def u_norm():
                    cqn = CQN[r][0].rearrange("p (k t) -> p k t", k=2)
                    for kc in range(2):
                        STT(cqn[:, kc, :], pb(kk[kc]), vecs[:, 5 + kc:6 + kc], RQ[r][0], ALU.mult, ALU.mult,
                            [PS.t[kk[kc]], RQ[r][1], t_vecs], [CQN[r][1]])
                    STT(CKVN[r][0], pb(kk[2]), vecs[:, 7:8], RKV[r][0], ALU.mult, ALU.mult, [PS.t[kk[2]], RKV[r][1], t_vecs], [CKVN[r][1]])

                U += [lambda: u_c(0), lambda: u_c(1), lambda: u_c(2), lambda: u_gate(0), lambda: u_gate(1), u_stats,
                      lambda: u_gate(2), lambda: u_gate(3), u_norm, u_kr]
                return U

            def B_units(g):
                gs = slice(g * 512, (g + 1) * 512)
                r = g % 2
                cqn = CQN[r][0].rearrange("p (k t) -> p k t", k=2)
                U = []

                def u_qn(h):
                    k = PS.next()
                    MM(k, pb(k), [(wuq[:, kc, h * 128:(h + 1) * 128], cqn[:, kc, :]) for kc in range(2)], [CQN[r][1], t_wuq])
                    COPY("act" if h % 2 == 0 else "dve", QT[:, h, gs], pb(k), [PS.t[k]], [QT_t[g]])

                def u_qr(pr):
                    kA = PS.next()
                    MM(kA, pb(kA), [(wuq[:, kc, 512 + pr * 128:512 + (pr + 1) * 128], cqn[:, kc, :]) for kc in range(2)], [CQN[r][1], t_wuq])
                    kB = PS.next()
                    MM(kB, pb(kB), [(wuq[:, kc, 768 + pr * 128:768 + (pr + 1) * 128], cqn[:, kc, :]) for kc in range(2)], [CQN[r][1], t_wuq])
                    rope_combine(kA, kB, [(0, 64, QR[0:64, 2 * pr, gs]), (64, 128, QR[64:128, 2 * pr + 1, gs])], QR_t[g], gs)

                def u_kn(h):
                    k = PS.next()
                    MM(k, pb(k), [(wukv[:, h * 128:(h + 1) * 128], CKVN[r][0])], [CKVN[r][1], t_wukv])
                    COPY("dve" if h % 2 == 0 else "act", KT[:, h, gs], pb(k), [PS.t[k]], [KT_t[g]])

                def u_v(i):
                    tt = 4 * g + i
                    k = PS.next()
                    MM(k, pb(k), [(CKVN[r][0][:, i * 128:(i + 1) * 128], wukv[:, 512:1024])], [CKVN[r][1], t_wukv])
                    COPY("act" if i % 2 == 0 else "dve", VM[:, tt, :], pb(k), [PS.t[k]], [VM_t[tt]])

                for h in range(4):
                    U.append(lambda h=h: u_qn(h))
                    U.append(lambda h=h: u_kn(h))
                    if h % 2 == 1:
                        U.append(lambda pr=h // 2: u_qr(pr))
                    U.append(lambda i=h: u_v(i))
                return U

            for u in A_units(0):
                u()
            for g in range(NG):
                Bu = B_units(g)
                Au = A_units(g + 1) if g + 1 < NG else []
                ia = ib = 0
                while ia < len(Au) or ib < len(Bu):
                    if ia < len(Au):
                        Au[ia]()
                        ia += 1
                    for _ in range(2 if ia < len(Au) else 1):
                        if ib < len(Bu):
                            Bu[ib]()
                            ib += 1
            PS.skip = set()
            dump("KT", KT, KT_t, [128, 4, SEQ], BF16)
            dump("QT", QT, QT_t, [128, 4, SEQ], BF16)
            dump("VM", VM, VM_t, [128, NT, 512], BF16)
            dump("KR", KR, KR_t, [128, SEQ], BF16)
            dump("QR", QR, QR_t, [128, 4, SEQ], BF16)
            dump("GM", GM, GM_t, [128, 4, SEQ], BF16)

        if stop_after >= 4:
            wout = WAR.ap(0, 16384, BF16).rearrange("p (k c) -> p k c", k=8)
            wout_t = WAR.tile("wout", 0, 16384, 5)
            DMA("pool", wout, w_out_d, None, writes=[wout_t])
            scr_off[0] = 0
            NPT = 12
            PT = [salloc(f"pt{i}", 1024, BF16, 4) for i in range(NPT)]
            RD = [salloc(f"rd{i}", 2048, F32, 4) for i in range(2)]
            TQ4 = [salloc(f"tq4{i}", 2048, F32, 4) for i in range(2)]
            D4 = salloc("den4hi", 1024, BF16, 4)
            D4L = salloc("den4lo", 1024, BF16, 4)
            obanks = [PS.next(), PS.next()]
            d4bank = PS.next()
            dbbank = PS.next()
            sbanks = [PS.next() for _ in range(4)]
            blocks = [(h, qg) for h in range(4) for qg in range(NG)]
            pairs_ = [(bi, kp) for bi in range(len(blocks)) for kp in range(NT // 2)]

            def issue_S_pair(pi):
                bi, kp = pairs_[pi]
                h, qg = blocks[bi]
                qs = slice(qg * 512, (qg + 1) * 512)
                b0, b1 = sbanks[(2 * pi) % 4], sbanks[(2 * pi + 1) % 4]
                k0 = slice((2 * kp) * 128, (2 * kp + 1) * 128)
                k1 = slice((2 * kp + 1) * 128, (2 * kp + 2) * 128)
                MMS([(pb(b0), KT[:, h, k0], QT[:, h, qs], True, False),
                     (pb(b0), KR[:, k0], QR[:, h, qs], False, True),
                     (pb(b1), KT[:, h, k1], QT[:, h, qs], True, False),
                     (pb(b1), KR[:, k1], QR[:, h, qs], False, True)],
                    [KT_t[(2 * kp) // 4], QT_t[qg], KR_t[(2 * kp) // 4], QR_t[qg]], [PS.t[b0], PS.t[b1]])

            issue_S_pair(0)
            n = 0
            pending = []
            for pi, (bi, kp) in enumerate(pairs_):
                h, qg = blocks[bi]
                if pi + 1 < len(pairs_):
                    issue_S_pair(pi + 1)
                ob = obanks[bi % 2]
                qs = slice(qg * 512, (qg + 1) * 512)
                for u in range(2):
                    kt = 2 * kp + u
                    p_ap, p_t = PT[n % NPT]
                    sbk = sbanks[(2 * pi + u) % 4]
                    ACT(p_ap, pb(sbk), AF.Exp, [PS.t[sbk]], [p_t], scale=ATT_SCALE)
                    MMS([(pb(ob), VM[:, kt, h * 128:(h + 1) * 128], p_ap, kt == 0, kt == NT - 1)], [p_t, VM_t[kt]], [PS.t[ob]])
                    n += 1
                if kp % 4 == 3:
                    items = []
                    rd_ = []
                    for q8 in range(2):
                        for jj in range(4):
                            p_ap, p_t = PT[(n - 8 + 4 * q8 + jj) % NPT]
                            items.append((pb(d4bank)[32 * jj:32 * (jj + 1), :], ones[:, 256:288], p_ap,
                                          (kp == 3 and q8 == 0), (kp == NT // 2 - 1 and q8 == 1), (0, 32 * jj)))
                            rd_.append(p_t)
                    MMS(items, rd_ + [t_const], [PS.t[d4bank]])
                if kp == NT // 2 - 1:
                    COPY("act", D4[0], pb(d4bank), [PS.t[d4bank]], [D4[1]])
                    TT("dve", D4L[0], pb(d4bank), D4[0], ALU.subtract, [PS.t[d4bank], D4[1]], [D4L[1]])
                    pending.append((bi, h, qg, ob))
                if (kp == 1 or pi == len(pairs_) - 1) and pending:
                    ebi, eh, eqg, eob = pending.pop(0)
                    eqs = slice(eqg * 512, (eqg + 1) * 512)
                    rd, rd_t = RD[ebi % 2]
                    tq, tq_t = TQ4[ebi % 2]
                    MM(dbbank, pb(dbbank), [(selb[:], D4[0]), (selb[:], D4L[0])], [D4[1], D4L[1], t_sel])
                    S.op("dve", lambda e, rd=rd: e.reciprocal(out=rd, in_=pb(dbbank)), reads=[PS.t[dbbank]], writes=[rd_t])
                    TT("dve", tq, pb(eob), rd, ALU.mult, [PS.t[eob], rd_t], [tq_t])
                    TT("dve", GM[:, eh, eqs], tq, GM[:, eh, eqs], ALU.mult, [tq_t, GM_t[eqg]], [GM_t[eqg]])
            dump("mixm", GM, GM_t, [128, 4, SEQ], BF16)

        if stop_after >= 5:
            scr_off[0] = 0
            lng, lng_t = salloc("lng", 4096, F32, 5)
            lnb, lnb_t = salloc("lnb", 4096, F32, 5)
            junk, junk_t = salloc("junk", 2048, BF16, 5)
            DMA("sp", lng, lngb_d[0:1, :].partition_broadcast(128), None, writes=[lng_t])
            DMA("sp", lnb, lngb_d[1:2, :].partition_broadcast(128), None, writes=[lnb_t])
            NR = 4
            XR = [(XTA.ap(i * 4096, 4096, F32), XTA.tile(f"xr{i}", i * 4096, 4096, 5)) for i in range(NR)]
            ZZ = [(XTA.ap(16384 + i * 4096, 4096, F32), XTA.tile(f"z{i}", 16384 + i * 4096, 4096, 5)) for i in range(NR)]
            ZH = [(XTA.tile(f"zh{i}a", 16384 + i * 4096, 2048, 5), XTA.tile(f"zh{i}b", 16384 + i * 4096 + 2048, 2048, 5)) for i in range(NR)]
            sem_xr = [S.dmasem(f"xr{i}") for i in range(NR)]
            sem_ob = [S.dmasem(f"ob{i}") for i in range(NR)]
            st_t = [Tile(f"lnst{i}") for i in range(NR)]
            outd_t = [Tile(f"outd{i}") for i in range(NR)]

            def load_xr(tt):
                DMA("sp", XR[tt % NR][0], x_d[tt * 128:(tt + 1) * 128, :], sem_xr[tt % NR], writes=[XR[tt % NR][1]])

            def st5_A(tt):
                sl = tt % NR
                ts_ = slice(tt * 128, (tt + 1) * 128)
                g = tt // 4
                xr, xr_t = XR[sl]
                z, z_t = ZZ[sl]
                ky = [PS.next(), PS.next()]
                for half in range(2):
                    pairs = []
                    for kc in range(8):
                        lhs = gateg[:, kc, ts_] if kc < 4 else GM[:, kc - 4, ts_]
                        pairs.append((lhs, wout[:, kc, half * 512:(half + 1) * 512]))
                    MM(ky[half], pb(ky[half]), pairs, [gateg_t[g], GM_t[g], wout_t])
                    hs = slice(half * 512, (half + 1) * 512)
                    STT(z[:, hs], xr[:, hs], ALPHA, pb(ky[half]), ALU.mult, ALU.add, [xr_t, PS.t[ky[half]]], [z_t, ZH[sl][0], ZH[sl][1]])
                st = lnst[:, sl, :]
                S.op("act", lambda e, st=st, z=z: e.activation(out=z, in_=z, func=AF.Identity, accum_out=st[:, 0:1]),
                     reads=[z_t], writes=[z_t, st_t[sl]])
                S.op("act", lambda e, st=st, z=z: e.activation(out=junk, in_=z, func=AF.Square, accum_out=st[:, 1:2]),
                     reads=[z_t], writes=[junk_t, st_t[sl]])

            def st5_B(tt):
                sl = tt % NR
                z, z_t = ZZ[sl]
                st = lnst[:, sl, :]
                TS("dve", st[:, 12:13], st[:, 0:1], 1.0 / D_MODEL, None, ALU.mult, None, [st_t[sl]], [st_t[sl]])
                TT("dve", st[:, 2:3], st[:, 12:13], st[:, 12:13], ALU.mult, [st_t[sl]], [st_t[sl]])
                STT(st[:, 13:14], st[:, 1:2], 1.0 / D_MODEL, st[:, 2:3], ALU.mult, ALU.subtract, [st_t[sl]], [st_t[sl]])
                ACT(st[:, 14:15], st[:, 13:14], AF.Ln, [st_t[sl]], [st_t[sl]], bias=LN_EPS)
                ACT(st[:, 14:15], st[:, 14:15], AF.Exp, [st_t[sl]], [st_t[sl]], scale=-0.5)
                TS("dve", st[:, 15:16], st[:, 12:13], -1.0, st[:, 14:15], ALU.mult, ALU.mult, [st_t[sl]], [st_t[sl]])
                ACT(z, z, AF.Identity, [z_t, st_t[sl]], [z_t], scale=st[:, 14:15], bias=st[:, 15:16])

            def st5_C(tt):
                sl = tt % NR
                ts_ = slice(tt * 128, (tt + 1) * 128)
                z, z_t = ZZ[sl]
                za_t, zb_t = ZH[sl]
                TT("pool", z, z, lng, ALU.mult, [z_t, lng_t], [z_t, za_t, zb_t])
                TT("dve", z[:, 512:1024], z[:, 512:1024], lnb[:, 512:1024], ALU.add, [zb_t, lnb_t], [zb_t])
                TT("pool", z[:, 0:512], z[:, 0:512], lnb[:, 0:512], ALU.add, [za_t, lnb_t], [za_t])
                DMA("sp", out_d[ts_, :], z, sem_ob[sl], reads=[z_t, za_t, zb_t], writes=[outd_t[sl]])
                if tt + NR < NT:
                    load_xr(tt + NR)

            for tt in range(NR):
                load_xr(tt)
            for it in range(NT + 2):
                if it < NT:
                    st5_A(it)
                if 0 <= it - 1 < NT:
                    st5_B(it - 1)
                if 0 <= it - 2 < NT:
                    st5_C(it - 2)
            dump_tiles.extend(outd_t)

        S.op("sp", lambda e: e.nop(), reads=dump_tiles)
        S.emit()
    return nc, dump_d


_PROGRAM = None


def kernel(**inputs):
    global _PROGRAM
    if _PROGRAM is None:
        _PROGRAM = build_program()[0]
    nc = _PROGRAM
    shared = prep_shared(inputs)
    x = np.asarray(inputs["x"], np.float32)
    pos = np.asarray(inputs["positions"]).astype(np.int32)
    in_maps = []
    for b in range(BATCH):
        m = dict(shared)
        m["x"] = np.ascontiguousarray(x[b])
        m["pos"] = np.ascontiguousarray(pos[b:b + 1])
        in_maps.append(m)
    res = run_bass_kernel_spmd(nc, in_maps, core_ids=list(range(BATCH)))
    return np.stack([np.asarray(r["out"], np.float32) for r in res.results], axis=0)
```
